# Optimizing a Trainium2 kernel written in Bass

```python
import math
import jax
import jax.numpy as jnp
from jax import lax
import numpy as np

D_MODEL = 4096
BATCH = 1
SEQ = 16384
DEPTH = 1
DEC_BATCH = 2
DEC_SEQ = 4096
PAST_LEN = 128

GRID_W = 64
Q_BLOCK = 128
ATT_WIDTH = D_MODEL // 2
HEAD_DIM = 128
N_Q_HEADS = ATT_WIDTH // HEAD_DIM
N_KV_HEADS = 4
Q_PER_KV = N_Q_HEADS // N_KV_HEADS
KV_WIDTH = N_KV_HEADS * HEAD_DIM
ROPE_SECTION = HEAD_DIM // 2
ROPE_THETA = 10000.0
ATT_SCALE = HEAD_DIM ** -0.5
SSM_WIDTH = D_MODEL // 4
SSM_GROUP = 16
N_SSM_GROUPS = SSM_WIDTH // SSM_GROUP
SSM_STATE = 64
DT_MIN = 1e-3
DT_MAX = 1e-1
N_MEM = 256
MEM_WIDTH = D_MODEL - ATT_WIDTH - SSM_WIDTH
N_MEM_HEADS = 4
MEM_HEAD_DIM = MEM_WIDTH // N_MEM_HEADS
MEM_SCALE = MEM_HEAD_DIM ** -0.5
IN_WIDTH = 2 * ATT_WIDTH + 2 * KV_WIDTH + 2 * SSM_WIDTH + 2 * MEM_WIDTH
SPLIT_POINTS = (ATT_WIDTH, ATT_WIDTH + KV_WIDTH, ATT_WIDTH + 2 * KV_WIDTH, 2 * ATT_WIDTH + 2 * KV_WIDTH, 2 * ATT_WIDTH + 2 * KV_WIDTH + SSM_WIDTH, 2 * ATT_WIDTH + 2 * KV_WIDTH + 2 * SSM_WIDTH, 2 * ATT_WIDTH + 2 * KV_WIDTH + 2 * SSM_WIDTH + MEM_WIDTH)
MIX_WIDTH = ATT_WIDTH + SSM_WIDTH + MEM_WIDTH
ALPHA = (2 * DEPTH) ** 0.25
BETA = (8 * DEPTH) ** -0.25
RMS_EPS = 1e-6
LN_EPS = 1e-5

kernel_name = 'hymba_gqa_s5_memxattn_deepnorm_encoder'


def _rms_heads(t, g):
    t = t.astype(jnp.float32)
    return t * lax.rsqrt(jnp.mean(t * t, axis=-1, keepdims=True) + RMS_EPS) * g.astype(jnp.float32)


def _axial_rope(L):
    rows = L // GRID_W
    row = jnp.broadcast_to(jnp.arange(rows, dtype=jnp.float32)[:, None], (rows, GRID_W)).reshape(L)
    col = jnp.broadcast_to(jnp.arange(GRID_W, dtype=jnp.float32)[None, :], (rows, GRID_W)).reshape(L)
    inv = ROPE_THETA ** (-jnp.arange(0, ROPE_SECTION, 2, dtype=jnp.float32) / ROPE_SECTION)
    ang_r = row[:, None] * inv[None, :]
    ang_c = col[:, None] * inv[None, :]
    ang = jnp.concatenate([ang_r, ang_r, ang_c, ang_c], axis=-1)
    return jnp.cos(ang)[:, None, :], jnp.sin(ang)[:, None, :]


def _apply_rope(t, cos, sin):
    sec = t.reshape(t.shape[:-1] + (2, 2, ROPE_SECTION // 2))
    rot = jnp.stack([-sec[..., 1, :], sec[..., 0, :]], axis=-2).reshape(t.shape)
    return t * cos + rot * sin


def _gqa_blocked(q, k, v):
    B, L = q.shape[0], q.shape[1]
    nb = L // Q_BLOCK
    qb = q.reshape(B, nb, Q_BLOCK, N_KV_HEADS, Q_PER_KV, HEAD_DIM).transpose(1, 0, 2, 3, 4, 5)

    def block(qblk):
        s = jnp.einsum('bqgrd,bkgd->bgrqk', qblk, k) * ATT_SCALE
        p = jax.nn.softmax(s, axis=-1)
        return jnp.einsum('bgrqk,bkgd->bqgrd', p, v)

    o = lax.map(block, qb)
    return o.transpose(1, 0, 2, 3, 4, 5).reshape(B, L, ATT_WIDTH)


def _s5_direction(u, lam_re, lam_im, log_step, b_re, b_im, reverse):
    f32 = jnp.float32
    lr = lam_re.astype(f32)
    li = lam_im.astype(f32)
    dt = jnp.exp(log_step.astype(f32))[:, None]
    mag = jnp.exp(lr * dt)
    ar = mag * jnp.cos(li * dt)
    ai = mag * jnp.sin(li * dt)
    den = lr * lr + li * li
    nr = ar - 1.0
    fr = (nr * lr + ai * li) / den
    fi = (ai * lr - nr * li) / den
    br = b_re.astype(f32)
    bi = b_im.astype(f32)
    bbar_r = fr[..., None] * br - fi[..., None] * bi
    bbar_i = fr[..., None] * bi + fi[..., None] * br
    bu_r = jnp.einsum('blgc,gpc->blgp', u, bbar_r)
    bu_i = jnp.einsum('blgc,gpc->blgp', u, bbar_i)
    a_r = jnp.broadcast_to(ar, bu_r.shape)
    a_i = jnp.broadcast_to(ai, bu_i.shape)

    def combine(e1, e2):
        a1r, a1i, b1r, b1i = e1
        a2r, a2i, b2r, b2i = e2
        return (a1r * a2r - a1i * a2i,
                a1r * a2i + a1i * a2r,
                a2r * b1r - a2i * b1i + b2r,
                a2r * b1i + a2i * b1r + b2i)

    _, _, xr, xi = lax.associative_scan(combine, (a_r, a_i, bu_r, bu_i), reverse=reverse, axis=1)
    return xr, xi


def _layer(x, mem, w_in, q_norm_g, k_norm_g, lam_re, lam_im, log_step, b_re, b_im,
           c_re, c_im, d_skip, w_glu, b_glu, w_mem_kv, w_out, ln_g, ln_b):
    f32 = jnp.float32
    B, L, _ = x.shape
    proj = jnp.einsum('bld,de->ble', x, w_in)
    q, k, v, g_att, u, g_ssm, q_mem, g_mem = jnp.split(proj, SPLIT_POINTS, axis=-1)

    cos, sin = _axial_rope(L)
    qh = _apply_rope(_rms_heads(q.reshape(B, L, N_Q_HEADS, HEAD_DIM), q_norm_g), cos, sin)
    kh = _apply_rope(_rms_heads(k.reshape(B, L, N_KV_HEADS, HEAD_DIM), k_norm_g), cos, sin)
    vh = v.reshape(B, L, N_KV_HEADS, HEAD_DIM).astype(f32)
    att = _gqa_blocked(qh, kh, vh) * jax.nn.silu(g_att.astype(f32))

    us = u.astype(f32).reshape(B, L, N_SSM_GROUPS, SSM_GROUP)
    xf_r, xf_i = _s5_direction(us, lam_re[0], lam_im[0], log_step[0], b_re[0], b_im[0], False)
    xb_r, xb_i = _s5_direction(us, lam_re[1], lam_im[1], log_step[1], b_re[1], b_im[1], True)
    y = (jnp.einsum('blgp,gcp->blgc', xf_r + xb_r, c_re.astype(f32))
         - jnp.einsum('blgp,gcp->blgc', xf_i + xb_i, c_im.astype(f32))
         + d_skip.astype(f32).reshape(N_SSM_GROUPS, SSM_GROUP) * us)
    y = jax.nn.gelu(y).reshape(B, L, SSM_WIDTH)
    ssm = y * jax.nn.sigmoid(y @ w_glu.astype(f32) + b_glu.astype(f32)) * jax.nn.silu(g_ssm.astype(f32))

    km, vm = jnp.split(jnp.einsum('bmd,de->bme', mem, w_mem_kv), 2, axis=-1)
    qm = q_mem.reshape(B, L, N_MEM_HEADS, MEM_HEAD_DIM).astype(f32)
    km = km.reshape(B, -1, N_MEM_HEADS, MEM_HEAD_DIM).astype(f32)
    vm = vm.reshape(B, -1, N_MEM_HEADS, MEM_HEAD_DIM).astype(f32)
    pm = jax.nn.softmax(jnp.einsum('blhd,bmhd->bhlm', qm, km) * MEM_SCALE, axis=-1)
    mem_out = jnp.einsum('bhlm,bmhd->blhd', pm, vm).reshape(B, L, MEM_WIDTH) * jax.nn.silu(g_mem.astype(f32))

    mixed = jnp.concatenate([att, ssm, mem_out], axis=-1).astype(x.dtype)
    h = ALPHA * x.astype(f32) + jnp.einsum('ble,ed->bld', mixed, w_out).astype(f32)
    mu = jnp.mean(h, axis=-1, keepdims=True)
    hc = h - mu
    var = jnp.mean(hc * hc, axis=-1, keepdims=True)
    out = hc * lax.rsqrt(var + LN_EPS) * ln_g.astype(f32) + ln_b.astype(f32)
    return out.astype(x.dtype)


def setup_inputs(seed: int = 0) -> dict:
    key = jax.random.key(seed)
    ks = jax.random.split(key, 24)
    f32 = jnp.float32
    nrm = lambda k, shape: jax.random.normal(k, shape, dtype=f32)
    x_prompt = nrm(ks[0], (BATCH, SEQ, D_MODEL))
    x_sample = nrm(ks[1], (DEC_BATCH, DEC_SEQ, D_MODEL))
    mem_prompt = nrm(ks[2], (BATCH, N_MEM, D_MODEL))
    mem_sample = nrm(ks[3], (DEC_BATCH, N_MEM, D_MODEL))
    w_in = nrm(ks[4], (DEPTH, D_MODEL, IN_WIDTH)) * D_MODEL ** -0.5
    q_norm_g = 1.0 + 0.02 * nrm(ks[5], (DEPTH, HEAD_DIM))
    k_norm_g = 1.0 + 0.02 * nrm(ks[6], (DEPTH, HEAD_DIM))
    ssm_lam_re = -0.5 + 0.01 * nrm(ks[7], (DEPTH, 2, N_SSM_GROUPS, SSM_STATE))
    ssm_lam_im = (math.pi * jnp.arange(SSM_STATE, dtype=f32)) + 0.01 * nrm(ks[8], (DEPTH, 2, N_SSM_GROUPS, SSM_STATE))
    ssm_log_step = jax.random.uniform(ks[9], (DEPTH, 2, N_SSM_GROUPS), dtype=f32, minval=math.log(DT_MIN), maxval=math.log(DT_MAX))
    ssm_b_re = nrm(ks[10], (DEPTH, 2, N_SSM_GROUPS, SSM_STATE, SSM_GROUP)) * (2 * SSM_GROUP) ** -0.5
    ssm_b_im = nrm(ks[11], (DEPTH, 2, N_SSM_GROUPS, SSM_STATE, SSM_GROUP)) * (2 * SSM_GROUP) ** -0.5
    ssm_c_re = nrm(ks[12], (DEPTH, N_SSM_GROUPS, SSM_GROUP, SSM_STATE)) * (2 * SSM_STATE) ** -0.5
    ssm_c_im = nrm(ks[13], (DEPTH, N_SSM_GROUPS, SSM_GROUP, SSM_STATE)) * (2 * SSM_STATE) ** -0.5
    ssm_d = nrm(ks[14], (DEPTH, SSM_WIDTH))
    w_glu = nrm(ks[15], (DEPTH, SSM_WIDTH, SSM_WIDTH)) * SSM_WIDTH ** -0.5
    b_glu = 0.01 * nrm(ks[16], (DEPTH, SSM_WIDTH))
    w_mem_kv = nrm(ks[17], (DEPTH, D_MODEL, 2 * MEM_WIDTH)) * D_MODEL ** -0.5
    w_out = nrm(ks[18], (DEPTH, MIX_WIDTH, D_MODEL)) * (MIX_WIDTH ** -0.5) * BETA
    ln_g = 1.0 + 0.02 * nrm(ks[19], (DEPTH, D_MODEL))
    ln_b = 0.02 * nrm(ks[20], (DEPTH, D_MODEL))
    return {'x_prompt': x_prompt, 'x_sample': x_sample, 'mem_prompt': mem_prompt, 'mem_sample': mem_sample,
            'w_in': w_in, 'q_norm_g': q_norm_g, 'k_norm_g': k_norm_g,
            'ssm_lam_re': ssm_lam_re, 'ssm_lam_im': ssm_lam_im, 'ssm_log_step': ssm_log_step,
            'ssm_b_re': ssm_b_re, 'ssm_b_im': ssm_b_im, 'ssm_c_re': ssm_c_re, 'ssm_c_im': ssm_c_im,
            'ssm_d': ssm_d, 'w_glu': w_glu, 'b_glu': b_glu, 'w_mem_kv': w_mem_kv, 'w_out': w_out,
            'ln_g': ln_g, 'ln_b': ln_b}


def reference(x_prompt, x_sample, mem_prompt, mem_sample, w_in, q_norm_g, k_norm_g,
              ssm_lam_re, ssm_lam_im, ssm_log_step, ssm_b_re, ssm_b_im, ssm_c_re, ssm_c_im,
              ssm_d, w_glu, b_glu, w_mem_kv, w_out, ln_g, ln_b):
    y_prompt = x_prompt
    y_sample = x_sample
    for l in range(DEPTH):
        y_prompt = _layer(y_prompt, mem_prompt, w_in[l], q_norm_g[l], k_norm_g[l],
                          ssm_lam_re[l], ssm_lam_im[l], ssm_log_step[l], ssm_b_re[l], ssm_b_im[l],
                          ssm_c_re[l], ssm_c_im[l], ssm_d[l], w_glu[l], b_glu[l], w_mem_kv[l],
                          w_out[l], ln_g[l], ln_b[l])
        y_sample = _layer(y_sample, mem_sample, w_in[l], q_norm_g[l], k_norm_g[l],
                          ssm_lam_re[l], ssm_lam_im[l], ssm_log_step[l], ssm_b_re[l], ssm_b_im[l],
                          ssm_c_re[l], ssm_c_im[l], ssm_d[l], w_glu[l], b_glu[l], w_mem_kv[l],
                          w_out[l], ln_g[l], ln_b[l])
    return (y_prompt, y_sample)
```

```python
import math
from contextlib import ExitStack
import numpy as np
import ml_dtypes
import concourse.bass as bass
import concourse.mybir as mybir
from concourse.bass_utils import run_bass_kernel_spmd

F32 = mybir.dt.float32
BF16 = mybir.dt.bfloat16
AF = mybir.ActivationFunctionType
ALU = mybir.AluOpType
NPBF = ml_dtypes.bfloat16

INW = 9216
NCORES = 8
TWO_PI = 2.0 * math.pi


class Tok:
    __slots__ = ("sem", "val", "eng", "kind")

    def __init__(self, sem, val, eng, kind):
        self.sem, self.val, self.eng, self.kind = sem, val, eng, kind


class KB:
    def __init__(self, nc):
        self.nc = nc
        self.E = {"pe": nc.tensor, "act": nc.scalar, "dve": nc.vector, "pool": nc.gpsimd, "sp": nc.sync}
        self.stack = ExitStack()
        ec = self.stack.enter_context
        self.csem = {e: ec(nc.semaphore("cs_" + e)) for e in ["pe", "act", "dve", "pool"]}
        self.ccnt = {e: 0 for e in self.csem}
        self.dpool = {q: [ec(nc.semaphore(f"ds_{q}{i}")) for i in range(n)] for q, n in [("sp", 16), ("pool", 8), ("act", 4)]}
        self.dcnt = {q: [0] * len(v) for q, v in self.dpool.items()}
        self.dn = {q: 0 for q in self.dpool}
        self.ccsem = ec(nc.semaphore("cc_sem"))
        self.cccnt = 0
        self.waited = {}
        self.lastw = {}
        self.readers = {}
        self.pe_cur = Tok(self.csem["pe"], None, "pe", "c")
        self.pe_unsig = 0
        self.stage_stack = None
        self.uid = 0

    def tile(self, name, shape, dtype):
        self.uid += 1
        return self.stage_stack.enter_context(self.nc.sbuf_tensor(f"{name}_{self.uid}", list(shape), dtype))

    def psum(self, name, shape, dtype):
        self.uid += 1
        return self.stage_stack.enter_context(self.nc.psum_tensor(f"{name}_{self.uid}", list(shape), dtype))

    def begin_stage(self):
        self.stage_stack = ExitStack()

    def end_stage(self):
        self.barrier()
        self.stage_stack.close()
        self.stage_stack = None

    def _wait(self, eng, tok):
        if tok.val is None:
            raise RuntimeError("dependency on unsignaled PE op")
        key = (eng, tok.sem.name)
        if self.waited.get(key, 0) >= tok.val:
            return
        self.E[eng].wait_ge(tok.sem, tok.val)
        self.waited[key] = tok.val

    def _deps(self, eng, reads, writes, is_dma):
        for k in reads:
            t = self.lastw.get(k)
            if t is not None and not (not is_dma and t.kind == "c" and t.eng == "pe" and eng == "pe"):
                self._wait(eng, t)
        for k in writes:
            t = self.lastw.get(k)
            if t is not None and not (not is_dma and t.kind == "c" and t.eng == "pe" and eng == "pe"):
                self._wait(eng, t)
            for r in self.readers.get(k, {}).values():
                if (not is_dma) and r.kind == "c" and r.eng == eng:
                    continue
                self._wait(eng, r)

    def _record(self, tok, reads, writes):
        for k in reads:
            self.readers.setdefault(k, {})[tok.sem.name] = tok
        for k in writes:
            self.lastw[k] = tok
            self.readers[k] = {}

    def op(self, eng, fn, reads=(), writes=(), sig=True):
        pr = [k for k in reads if k.startswith("P_")]
        if pr:
            reads = [k for k in reads if not k.startswith("P_")]
            writes = list(writes) + [k for k in pr if k not in writes]
        self._deps(eng, reads, writes, False)
        ins = fn(self.E[eng])
        if eng == "pe":
            tok = self.pe_cur
            if sig:
                self.ccnt["pe"] += 1
                ins.then_inc(self.csem["pe"], 1)
                tok.val = self.ccnt["pe"]
                self.pe_cur = Tok(self.csem["pe"], None, "pe", "c")
                self.pe_unsig = 0
            else:
                self.pe_unsig += 1
        else:
            self.ccnt[eng] += 1
            ins.then_inc(self.csem[eng], 1)
            tok = Tok(self.csem[eng], self.ccnt[eng], eng, "c")
        self._record(tok, reads, writes)
        return tok

    def dma(self, q, out, in_, reads=(), writes=()):
        self._deps(q, reads, writes, True)
        i = self.dn[q] % len(self.dpool[q])
        self.dn[q] += 1
        self.dcnt[q][i] += 16
        ins = self.E[q].dma_start(out=out, in_=in_)
        ins.then_inc(self.dpool[q][i], 16)
        tok = Tok(self.dpool[q][i], self.dcnt[q][i], q, "d")
        self._record(tok, reads, writes)
        return tok

    def allgather(self, in_ap, out_ap, groups, reads=(), writes=()):
        self._deps("pool", reads, writes, True)
        self.cccnt += 1
        ins = self.E["pool"].collective_compute("AllGather", ALU.bypass, replica_groups=groups, ins=[in_ap], outs=[out_ap])
        ins.then_inc(self.ccsem, 1)
        tok = Tok(self.ccsem, self.cccnt, "pool", "d")
        self._record(tok, reads, writes)
        return tok

    def barrier(self):
        assert self.pe_unsig == 0, "last PE op before barrier must be signaled"
        toks = [Tok(self.csem[e], self.ccnt[e], e, "c") for e in self.csem if self.ccnt[e] > 0]
        for q in self.dpool:
            for i, s in enumerate(self.dpool[q]):
                if self.dcnt[q][i] > 0:
                    toks.append(Tok(s, self.dcnt[q][i], q, "d"))
        if self.cccnt:
            toks.append(Tok(self.ccsem, self.cccnt, "pool", "d"))
        for eng in ["pe", "act", "dve", "pool", "sp"]:
            for t in toks:
                self._wait(eng, t)
        self.lastw.clear()
        self.readers.clear()


def dram_bcast(handle, nparts, n, offset=0):
    return bass.AP(handle, offset, [[0, nparts], [1, n]])


class Cfg:
    def __init__(self, LP, LS, D=4096):
        self.LP, self.LS, self.D = LP, LS, D
        self.NP = LP // 8
        self.NS = LS // 4
        self.NT = self.NP + self.NS
        self.TT = min(512, self.NP, self.NS)
        assert self.NP % self.TT == 0 and self.NS % self.TT == 0 and self.TT % 128 == 0


def build(cfg, debug=False, zero_rest=False, stop=None):
    NP, NS, NT, TT, LP, LS = cfg.NP, cfg.NS, cfg.NT, cfg.TT, cfg.LP, cfg.LS
    SUB = TT // 128
    D = cfg.D
    KC = D // 128
    nc = bass.Bass("TRN2", target_bir_lowering=False)

    def din(name, shape, dt=F32):
        return nc.dram_tensor(name, list(shape), dt, kind="ExternalInput")

    def dscr(name, shape, dt=BF16):
        return nc.dram_tensor(name, list(shape), dt)

    x_own = din("x_own", [NT, D])
    mem_p = din("mem_p", [256, D])
    mem_s = din("mem_s", [256, D])
    w_in = din("w_in", [D, INW])
    w_out = din("w_out", [4096, D])
    w_glu = din("w_glu", [1024, 1024])
    w_mkv = din("w_mkv", [D, 2048])
    qg_in = din("qg", [128, 1])
    kg_in = din("kg", [128, 1])
    lam_re = din("lam_re", [64, 128])
    lam_im = din("lam_im", [64, 128])
    log_step = din("log_step", [1, 128])
    b_re = din("b_re", [2, 64, 64, 16])
    b_im = din("b_im", [2, 64, 64, 16])
    c_re = din("c_re", [64, 16, 64])
    c_im = din("c_im", [64, 16, 64])
    ssm_d = din("ssm_d", [32, 32])
    b_glu = din("b_glu", [8, 128])
    ln_g = din("ln_g", [1, D])
    ln_b = din("ln_b", [1, D])
    pos_row = din("pos_row", [1, NT])
    pos_col = din("pos_col", [1, NT])
    inv_col = din("inv_col", [128, 1])
    ident_bf_in = din("ident_bf", [128, 128], BF16)
    ident_f_in = din("ident_f", [128, 128])
    rot_in = din("rotm", [128, 128], BF16)
    selp = din("selp", [1, 16])
    sels = din("sels", [1, 8])

    y_out = nc.dram_tensor("y_out", [NT, D], F32, kind="ExternalOutput")

    w_in_bf = dscr("w_in_bf", [D, INW])
    w_out_bf = dscr("w_out_bf", [4096, D])
    w_glu_bf = dscr("w_glu_bf", [1024, 1024])
    w_mkv_bf = dscr("w_mkv_bf", [D, 2048])
    qT = dscr("qT", [2048, NT])
    kTp_own = dscr("kTp_own", [512, NP])
    kTs_own = dscr("kTs_own", [512, NS])
    vp_own = dscr("vp_own", [NP, 512])
    vs_own = dscr("vs_own", [NS, 512])
    kTp_all = dscr("kTp_all", [8 * 512, NP])
    kTs_all = dscr("kTs_all", [4 * 512, NS])
    vp_all = dscr("vp_all", [8 * NP, 512])
    vs_all = dscr("vs_all", [4 * NS, 512])
    sgatt = dscr("sgatt", [2048, NT])
    uT = dscr("uT", [1024, NT])
    sgssm = dscr("sgssm", [1024, NT])
    qmT = dscr("qmT", [1024, NT])
    sgmem = dscr("sgmem", [1024, NT])
    mixT = dscr("mixT", [4096, NT])
    dbg = {}
    if debug:
        for nm, src in [("qT", qT), ("kTp_all", kTp_all), ("vp_all", vp_all), ("mixT", mixT), ("sgatt", sgatt), ("uT", uT)]:
            dbg[nm] = (nc.dram_tensor("dbg_" + nm, list(src.shape), BF16, kind="ExternalOutput"), src)

    kb = KB(nc)
    ALL8 = [list(range(8))]
    GRP4 = [[0, 1, 2, 3], [4, 5, 6, 7]]

    kb.begin_stage()
    for dst, src, rows, ch in [(w_in_bf, w_in, D, 128), (w_out_bf, w_out, 4096, 256), (w_glu_bf, w_glu, 1024, 256), (w_mkv_bf, w_mkv, D, 512)]:
        for r0 in range(0, rows, ch):
            kb.dma("pool", dst.ap()[r0:r0 + ch, :], src.ap()[r0:r0 + ch, :])
    kb.end_stage()

    if stop == "0":
        kb.stack.close()
        return nc
    kb.begin_stage()
    ident_bf = kb.tile("ident_bf", [128, 128], BF16)
    rotm = kb.tile("rotm", [128, 128], BF16)
    ones_bf = kb.tile("ones_bf", [128, 128], BF16)
    qcol = kb.tile("qcol", [128, 1], F32)
    kcol = kb.tile("kcol", [128, 1], F32)
    invc = kb.tile("invc", [128, 1], F32)
    cosT = kb.tile("cosT", [128, NT], F32)
    sinT = kb.tile("sinT", [128, NT], F32)
    angT = kb.tile("angT", [128, NT], F32)
    kb.dma("sp", ident_bf[:], ident_bf_in.ap(), writes=["ident"])
    kb.dma("sp", rotm[:], rot_in.ap(), writes=["rotm"])
    kb.dma("sp", qcol[:], qg_in.ap(), writes=["qcol"])
    kb.dma("sp", kcol[:], kg_in.ap(), writes=["kcol0"])
    kb.dma("sp", invc[:], inv_col.ap(), writes=["invc"])
    kb.dma("sp", cosT[0:64, :], dram_bcast(pos_row, 64, NT), writes=["posA"])
    kb.dma("sp", cosT[64:128, :], dram_bcast(pos_col, 64, NT), writes=["posB"])
    kb.op("pool", lambda e: e.memset(ones_bf[:], 1.0), writes=["ones"])
    epsq = kb.tile("epsq", [128, 1], F32)
    kb.op("pool", lambda e: e.memset(epsq[:], 128.0 * 1e-6), writes=["epsq"])
    kb.op("dve", lambda e: e.tensor_scalar(out=kcol[:], in0=kcol[:], scalar1=math.sqrt(128.0), scalar2=None, op0=ALU.mult),
          reads=["kcol0"], writes=["kcol"])
    yT = kb.tile("yT", [128, NT], F32)
    kiT = kb.tile("kiT", [128, NT], mybir.dt.int32)
    kb.op("dve", lambda e: e.tensor_scalar(out=yT[:], in0=cosT[:], scalar1=invc[:, 0:1], scalar2=1.0 / TWO_PI, op0=ALU.mult, op1=ALU.mult),
          reads=["posA", "posB", "invc"], writes=["yT"])

    def trig_table(dst, shift, tag):
        kb.op("dve", lambda e: e.tensor_scalar(out=angT[:], in0=yT[:], scalar1=shift, scalar2=None, op0=ALU.add), reads=["yT", "trig"], writes=["angT"])
        kb.op("dve", lambda e: e.tensor_copy(out=kiT[:], in_=angT[:]), reads=["angT"], writes=["kiT"])
        kb.op("dve", lambda e: e.tensor_copy(out=dst[:], in_=kiT[:]), reads=["kiT"], writes=["kfT"])
        kb.op("dve", lambda e: e.tensor_tensor(out=angT[:], in0=angT[:], in1=dst[:], op=ALU.subtract), reads=["angT", "kfT"], writes=["angT"])
        kb.op("dve", lambda e: e.scalar_tensor_tensor(out=dst[:], in0=angT[:], scalar=0.5, in1=angT[:], op0=ALU.is_gt, op1=ALU.subtract), reads=["angT"], writes=["kfT"])
        kb.op("dve", lambda e: e.scalar_tensor_tensor(out=angT[:], in0=dst[:], scalar=0.5, in1=dst[:], op0=ALU.is_gt, op1=ALU.subtract), reads=["kfT"], writes=["angT"])
        kb.op("act", lambda e: e.activation(out=dst[:], in_=angT[:], func=AF.Sin, scale=TWO_PI), reads=["angT"], writes=[tag, "trig"])

    trig_table(sinT, 0.0, "sinT")
    trig_table(cosT, 0.25, "cosT")

    if stop == "A0":
        kb.end_stage(); kb.stack.close()
        return nc
    xb = kb.tile("xb", [128, SUB, D], BF16)
    xT = kb.tile("xT", [128, KC, TT], BF16)
    wbuf = [kb.tile(f"wbuf{i}", [128, KC, 512], BF16) for i in range(2)]
    pst = [kb.psum(f"pst{i}", [128, 1024], BF16) for i in range(2)]
    acc = [kb.psum(f"acc{i}", [128, 512], F32) for i in range(2)]
    ssps = kb.psum("P_ssps", [128, 512], F32)
    rotps = kb.psum("P_rotps", [128, 512], F32)
    sq = [kb.tile(f"sq{i}", [128, TT], BF16) for i in range(2)]
    qgb = [kb.tile(f"qgb{i}", [128, TT], BF16) for i in range(2)]
    rr = [kb.tile(f"rr{i}", [128, TT], F32) for i in range(2)]
    At = [kb.tile(f"At{i}", [128, TT], F32) for i in range(2)]
    Bt = [kb.tile(f"Bt{i}", [128, TT], F32) for i in range(1)]
    ob = [kb.tile(f"ob{i}", [128, 512], BF16) for i in range(3)]
    w_in_v = w_in_bf.ap().rearrange("(k p) c -> p k c", p=128)

    tiles = [(t0, 0) for t0 in range(0, NP, TT)] + [(t0, 1) for t0 in range(NP, NT, TT)]
    nacc = 0
    nob = 0
    nqk = 0
    ngrp = 0
    for (t0, seg) in tiles:
        for s in range(SUB):
            kb.dma("pool", xb[:, s, :], x_own.ap()[t0 + 128 * s: t0 + 128 * (s + 1), :], writes=[f"xb{s}"])
        for k in range(KC):
            pb = k % 2
            for s in range(SUB):
                kb.op("pe", lambda e, k=k, s=s, pb=pb: e.transpose(out=pst[pb][:, 128 * s:128 * (s + 1)], in_=xb[:, s, 128 * k:128 * (k + 1)], identity=ident_bf[:]),
                      reads=[f"xb{s}", "ident"], writes=[f"P_pst{pb}"], sig=(s == SUB - 1))
            eng = "dve" if k % 2 == 0 else "act"
            if eng == "dve":
                kb.op("dve", lambda e, k=k, pb=pb: e.tensor_copy(out=xT[:, k, :], in_=pst[pb][:, 0:TT]), reads=[f"P_pst{pb}"], writes=[f"xT{k}"])
            else:
                kb.op("act", lambda e, k=k, pb=pb: e.activation(out=xT[:, k, :], in_=pst[pb][:, 0:TT], func=AF.Copy), reads=[f"P_pst{pb}"], writes=[f"xT{k}"])
        if stop == "A1":
            kb.end_stage(); kb.stack.close()
            return nc
        for g in range(18):
            wb = wbuf[ngrp % 2]
            wkey = f"wbuf{ngrp % 2}"
            if ngrp == 0:
                kb.dma("sp", wb[:], w_in_v[:, :, 0:512], writes=[wkey])
            ngrp += 1
            if not (g == 17 and (t0, seg) == tiles[-1]):
                gn = (g + 1) % 18
                kb.dma("sp", wbuf[ngrp % 2][:], w_in_v[:, :, 512 * gn:512 * (gn + 1)], writes=[f"wbuf{ngrp % 2}"])
            if g == 5:
                for s in range(SUB):
                    a = acc[nacc % 2]
                    akey = f"P_acc{nacc % 2}"
                    nacc += 1
                    for k in range(KC):
                        kb.op("pe", lambda e, a=a, k=k, s=s, wb=wb: e.matmul(a[:], lhsT=xT[:, k, 128 * s:128 * (s + 1)], rhs=wb[:, k, :], start=(k == 0), stop=(k == KC - 1)),
                              reads=[f"xT{k}", wkey], writes=[akey], sig=(k == KC - 1))
                    o = ob[nob % 3]
                    okey = f"ob{nob % 3}"
                    nob += 1
                    kb.op("dve", lambda e, a=a, o=o: e.tensor_copy(out=o[:], in_=a[:]), reads=[akey], writes=[okey])
                    r0 = t0 + 128 * s
                    dst = vp_own.ap()[r0:r0 + 128, :] if seg == 0 else vs_own.ap()[r0 - NP:r0 - NP + 128, :]
                    kb.dma("sp", dst, o[:], reads=[okey])
                continue
            for j in range(4):
                col0 = 512 * g + 128 * j
                a = acc[nacc % 2]
                akey = f"P_acc{nacc % 2}"
                nacc += 1
                for k in range(KC):
                    kb.op("pe", lambda e, a=a, k=k, j=j, wb=wb: e.matmul(a[:, 0:TT], lhsT=wb[:, k, 128 * j:128 * (j + 1)], rhs=xT[:, k, :], start=(k == 0), stop=(k == KC - 1)),
                          reads=[f"xT{k}", wkey], writes=[akey], sig=(k == KC - 1))
                o = ob[nob % 3]
                okey = f"ob{nob % 3}"
                nob += 1
                if g <= 4:
                    i2 = nqk % 2
                    nqk += 1
                    gcol = qcol if g < 4 else kcol
                    gkey = "qcol" if g < 4 else "kcol"
                    kb.op("act", lambda e, a=a, i2=i2: e.activation(out=sq[i2][:], in_=a[:, 0:TT], func=AF.Square), reads=[akey], writes=[f"sq{i2}"])
                    kb.op("act", lambda e, a=a, i2=i2, gcol=gcol: e.activation(out=qgb[i2][:], in_=a[:, 0:TT], func=AF.Copy, scale=gcol[:, 0:1]),
                          reads=[akey, gkey], writes=[f"qgb{i2}"])
                    kb.op("pe", lambda e, i2=i2: e.matmul(ssps[:, 0:TT], lhsT=ones_bf[:], rhs=sq[i2][:], start=True, stop=True),
                          reads=[f"sq{i2}", "ones"], writes=["P_ssps"])
                    kb.op("pe", lambda e, i2=i2: e.matmul(rotps[:, 0:TT], lhsT=rotm[:], rhs=qgb[i2][:], start=True, stop=True),
                          reads=[f"qgb{i2}", "rotm"], writes=["P_rotps"])
                    kb.op("act", lambda e, i2=i2: e.activation(out=rr[i2][:], in_=ssps[:, 0:TT], func=AF.Sqrt, bias=epsq[:, 0:1]), reads=["P_ssps", "epsq"], writes=[f"rr{i2}"])
                    kb.op("dve", lambda e, i2=i2: e.reciprocal(out=rr[i2][:], in_=rr[i2][:]), reads=[f"rr{i2}"], writes=[f"rr{i2}"])
                    kb.op("dve", lambda e, a=a, i2=i2, gcol=gcol: e.scalar_tensor_tensor(out=At[i2][:], in0=a[:, 0:TT], scalar=gcol[:, 0:1], in1=cosT[:, t0:t0 + TT], op0=ALU.mult, op1=ALU.mult),
                          reads=[akey, gkey, "cosT"], writes=[f"At{i2}"])
                    kb.op("dve", lambda e, i2=i2: e.tensor_tensor(out=Bt[0][:], in0=rotps[:, 0:TT], in1=sinT[:, t0:t0 + TT], op=ALU.mult),
                          reads=["P_rotps", "sinT"], writes=["Bt0"])
                    kb.op("pool", lambda e, i2=i2: e.tensor_tensor(out=At[i2][:], in0=At[i2][:], in1=Bt[0][:], op=ALU.add),
                          reads=[f"At{i2}", "Bt0"], writes=[f"At{i2}"])
                    kb.op("pool", lambda e, i2=i2, o=o: e.tensor_tensor(out=o[:, 0:TT], in0=At[i2][:], in1=rr[i2][:], op=ALU.mult),
                          reads=[f"At{i2}", f"rr{i2}"], writes=[okey])
                    if g < 4:
                        dst = qT.ap()[col0:col0 + 128, t0:t0 + TT]
                        wk = "qT"
                    else:
                        r0 = 128 * j
                        dst = kTp_own.ap()[r0:r0 + 128, t0:t0 + TT] if seg == 0 else kTs_own.ap()[r0:r0 + 128, t0 - NP:t0 - NP + TT]
                        wk = "k_own"
                    kb.dma("sp", dst, o[:, 0:TT], reads=[okey])
                    continue
                if 6 <= g <= 9:
                    dst, fn, sc = sgatt.ap()[col0 - 3072:col0 - 3072 + 128, t0:t0 + TT], AF.Silu, 1.0
                elif 10 <= g <= 11:
                    dst, fn, sc = uT.ap()[col0 - 5120:col0 - 5120 + 128, t0:t0 + TT], AF.Copy, 1.0
                elif 12 <= g <= 13:
                    dst, fn, sc = sgssm.ap()[col0 - 6144:col0 - 6144 + 128, t0:t0 + TT], AF.Silu, 1.0
                elif 14 <= g <= 15:
                    dst, fn, sc = qmT.ap()[col0 - 7168:col0 - 7168 + 128, t0:t0 + TT], AF.Copy, 1.0 / 16.0
                else:
                    dst, fn, sc = sgmem.ap()[col0 - 8192:col0 - 8192 + 128, t0:t0 + TT], AF.Silu, 1.0
                if fn == AF.Copy:
                    kb.op("dve", lambda e, a=a, o=o, sc=sc: e.tensor_scalar(out=o[:, 0:TT], in0=a[:, 0:TT], scalar1=sc, scalar2=None, op0=ALU.mult), reads=[akey], writes=[okey])
                else:
                    kb.op("act", lambda e, a=a, o=o, fn=fn: e.activation(out=o[:, 0:TT], in_=a[:, 0:TT], func=fn), reads=[akey], writes=[okey])
                kb.dma("sp", dst, o[:, 0:TT], reads=[okey])
    kb.end_stage()

    if stop == "A":
        kb.stack.close()
        return nc
    kb.begin_stage()
    kb.allgather(kTp_own.ap(), kTp_all.ap(), ALL8)
    kb.allgather(vp_own.ap(), vp_all.ap(), ALL8)
    kb.allgather(kTs_own.ap(), kTs_all.ap(), GRP4)
    kb.allgather(vs_own.ap(), vs_all.ap(), GRP4)
    kb.end_stage()

    if stop == "AG":
        kb.stack.close()
        return nc
    kb.begin_stage()
    ones_bf = kb.tile("ones_bf", [128, 128], BF16)
    kb.op("pool", lambda e: e.memset(ones_bf[:], 1.0), writes=["ones"])
    LMAX = max(LP, LS)
    KTs = [kb.tile(f"KT{i}", [128, LMAX], BF16) for i in range(2)]
    VVs = [kb.tile(f"VV{i}", [128, LMAX // 128, 128], BF16) for i in range(2)]
    qb = [kb.tile(f"qb{i}", [128, TT], BF16) for i in range(2)]
    gb = [kb.tile(f"gb{i}", [128, TT], BF16) for i in range(2)]
    NPT = 6
    Pt = [kb.tile(f"Pt{i}", [128, TT], BF16) for i in range(NPT)]
    accD = [kb.tile(f"accD{i}", [128, TT], F32) for i in range(2)]
    accP = [kb.tile(f"accP{i}", [128, TT], F32) for i in range(2)]
    ones_f = kb.tile("ones_f", [128, 128], F32)
    kb.op("pool", lambda e: e.memset(ones_f[:], 1.0), writes=["ones_f"])
    rec = kb.tile("rec", [128, TT], F32)
    ot = kb.tile("ot", [128, TT], F32)
    ob2 = [kb.tile(f"ob2{i}", [128, TT], BF16) for i in range(2)]
    Sps = [kb.psum(f"Sps{i}", [128, 512], F32) for i in range(2)]
    Ops = [kb.psum(f"Ops{i}", [128, 512], F32) for i in range(2)]
    Lps = [kb.psum(f"Lps{i}", [128, 512], F32) for i in range(2)]
    nS = 0
    nP = 0
    heads = [(seg, h) for seg in range(2) for h in range(4)]
    blocks = []
    for hi, (seg, h) in enumerate(heads):
        nown = NP if seg == 0 else NS
        tbase = 0 if seg == 0 else NP
        for hq in range(4 * h, 4 * h + 4):
            for qb0 in range(0, nown, TT):
                blocks.append((hi, hq, tbase + qb0))

    def load_kv(hi):
        seg, h = heads[hi]
        L = LP if seg == 0 else LS
        nranks = 8 if seg == 0 else 4
        kall = kTp_all if seg == 0 else kTs_all
        vall = vp_all if seg == 0 else vs_all
        kb.dma("sp", KTs[hi % 2][:, 0:L].rearrange("p (r t) -> p r t", r=nranks),
               kall.ap().rearrange("(r hd) t -> hd r t", r=nranks)[128 * h:128 * (h + 1), :, :], writes=[f"KT{hi % 2}"])
        kb.dma("sp", VVs[hi % 2][:, 0:L // 128, :], vall.ap().rearrange("(kt p) c -> p kt c", p=128)[:, :, 128 * h:128 * (h + 1)], writes=[f"VV{hi % 2}"])

    def load_qg(bi_):
        _, hq_, tq_ = blocks[bi_]
        kb.dma("sp", qb[bi_ % 2][:], qT.ap()[128 * hq_:128 * (hq_ + 1), tq_:tq_ + TT], writes=[f"qb{bi_ % 2}"])
        kb.dma("sp", gb[bi_ % 2][:], sgatt.ap()[128 * hq_:128 * (hq_ + 1), tq_:tq_ + TT], writes=[f"gb{bi_ % 2}"])

    load_kv(0)
    load_qg(0)
    for nblk, (hi, hq, tq) in enumerate(blocks):
        seg, h = heads[hi]
        nkt = (LP if seg == 0 else LS) // 128
        first_of_head = (nblk == 0 or blocks[nblk - 1][0] != hi)
        if first_of_head and hi + 1 < len(heads):
            load_kv(hi + 1)
        if nblk + 1 < len(blocks):
            load_qg(nblk + 1)
        bi = nblk % 2
        KT, VV = KTs[hi % 2], VVs[hi % 2]
        ktk, vvk = f"KT{hi % 2}", f"VV{hi % 2}"
        qt, gt = qb[bi], gb[bi]
        O, Lp = Ops[bi], Lps[bi]

        def emit_S(kt):
            nonlocal nS
            si = nS % 2
            nS += 1
            kb.op("pe", lambda e: e.matmul(Sps[si][:, 0:TT], lhsT=KT[:, 128 * kt:128 * (kt + 1)], rhs=qt[:], start=True, stop=True),
                  reads=[ktk, f"qb{bi}"], writes=[f"P_Sps{si}"])
            return si

        def emit_PV(kt, si):
            nonlocal nP
            pi = nP % NPT
            nP += 1
            kb.op("act", lambda e: e.activation(out=Pt[pi][:], in_=Sps[si][:, 0:TT], func=AF.Exp), reads=[f"P_Sps{si}"], writes=[f"Pt{pi}"])
            kb.op("pe", lambda e: e.matmul(O[:, 0:TT], lhsT=VV[:, kt, :], rhs=Pt[pi][:], start=(kt == 0), stop=(kt == nkt - 1)),
                  reads=[vvk, f"Pt{pi}"], writes=[f"P_Ops{bi}"], sig=(kt == nkt - 1))
            PR = 6 if nkt >= 6 else 3
            if kt % PR == PR - 1:
                eng_, acc_, akey_, first_ = "pool", accP[bi], f"accP{bi}", (kt == PR - 1)
            else:
                eng_, acc_, akey_, first_ = "dve", accD[bi], f"accD{bi}", (kt == 0)
            if first_:
                kb.op(eng_, lambda e: e.tensor_copy(out=acc_[:], in_=Pt[pi][:]), reads=[f"Pt{pi}"], writes=[akey_])
            else:
                kb.op(eng_, lambda e: e.tensor_tensor(out=acc_[:], in0=acc_[:], in1=Pt[pi][:], op=ALU.add), reads=[f"Pt{pi}", akey_], writes=[akey_])

        prev = emit_S(0)
        for kt in range(nkt):
            nxt = emit_S(kt + 1) if kt + 1 < nkt else None
            emit_PV(kt, prev)
            prev = nxt
        oi = nblk % 2
        kb.op("pe", lambda e: e.matmul(Lp[:, 0:TT], lhsT=ones_f[:], rhs=accD[bi][:], start=True, stop=False),
              reads=["ones_f", f"accD{bi}"], writes=[f"P_Lps{bi}"], sig=False)
        kb.op("pe", lambda e: e.matmul(Lp[:, 0:TT], lhsT=ones_f[:], rhs=accP[bi][:], start=False, stop=True),
              reads=["ones_f", f"accP{bi}"], writes=[f"P_Lps{bi}"], sig=True)
        kb.op("dve", lambda e: e.reciprocal(out=rec[:], in_=Lp[:, 0:TT]), reads=[f"P_Lps{bi}"], writes=["rec"])
        kb.op("dve", lambda e: e.tensor_tensor(out=ot[:], in0=O[:, 0:TT], in1=rec[:], op=ALU.mult), reads=[f"P_Ops{bi}", "rec"], writes=["ot"])
        kb.op("pool", lambda e: e.tensor_tensor(out=ob2[oi][:], in0=ot[:], in1=gt[:], op=ALU.mult), reads=["ot", f"gb{bi}"], writes=[f"ob2{oi}"])
        kb.dma("sp", mixT.ap()[128 * hq:128 * (hq + 1), tq:tq + TT], ob2[oi][:], reads=[f"ob2{oi}"])
    kb.end_stage()


    if zero_rest != True:
        kb.begin_stage()
        ones_bf = kb.tile("ones_bf", [128, 128], BF16)
        ident_bf = kb.tile("ident_bf", [128, 128], BF16)
        kb.op("pool", lambda e: e.memset(ones_bf[:], 1.0), writes=["ones"])
        kb.dma("sp", ident_bf[:], ident_bf_in.ap(), writes=["ident"])
        memb = kb.tile("memb", [128, 2, D], BF16)
        memT = kb.tile("memT", [128, KC, 256], BF16)
        wmk = [kb.tile(f"wmk{i}", [128, KC, 512], BF16) for i in range(2)]
        kmT = kb.tile("kmT", [128, 8, 256], BF16)
        vmt = kb.tile("vmt", [128, 2, 1024], BF16)
        qmt = kb.tile("qmt", [128, 8, TT], BF16)
        sgm = kb.tile("sgm", [128, 8, TT], BF16)
        Pm = [kb.tile(f"Pm{i}", [128, TT], BF16) for i in range(4)]
        recm = kb.tile("recm", [128, TT], F32)
        otm = kb.tile("otm", [128, TT], F32)
        obm = [kb.tile(f"obm{i}", [128, TT], BF16) for i in range(2)]
        mbank = [kb.psum(f"mb{i}", [128, 512], F32) for i in range(6)]
        tbank = kb.psum("mtb", [128, 1024], BF16)
        nmb = [0]

        def nb():
            i = nmb[0] % len(mbank)
            nmb[0] += 1
            return mbank[i], f"P_mb{i}"

        w_mkv_v = w_mkv_bf.ap().rearrange("(k p) c -> p k c", p=128)
        nwm = 0
        npm = 0
        nobm = 0
        for seg in range(2):
            memsrc = mem_p if seg == 0 else mem_s
            for s2 in range(2):
                kb.dma("pool", memb[:, s2, :], memsrc.ap()[128 * s2:128 * (s2 + 1), :], writes=[f"memb{s2}"])
            for k in range(KC):
                for s2 in range(2):
                    kb.op("pe", lambda e: e.transpose(out=tbank[:, 128 * s2:128 * (s2 + 1)], in_=memb[:, s2, 128 * k:128 * (k + 1)], identity=ident_bf[:]),
                          reads=[f"memb{s2}", "ident"], writes=["P_mtb"], sig=(s2 == 1))
                kb.op("dve", lambda e: e.tensor_copy(out=memT[:, k, :], in_=tbank[:, 0:256]), reads=["P_mtb"], writes=["memT"])
            for g in range(4):
                w = wmk[nwm % 2]
                wkey = f"wmk{nwm % 2}"
                nwm += 1
                kb.dma("sp", w[:], w_mkv_v[:, :, 512 * g:512 * (g + 1)], writes=[wkey])
                if g < 2:
                    for j in range(4):
                        bk, bkey = nb()
                        for k in range(KC):
                            kb.op("pe", lambda e: e.matmul(bk[:, 0:256], lhsT=w[:, k, 128 * j:128 * (j + 1)], rhs=memT[:, k, :], start=(k == 0), stop=(k == KC - 1)),
                                  reads=["memT", wkey], writes=[bkey], sig=(k == KC - 1))
                        kb.op("act", lambda e: e.activation(out=kmT[:, 4 * g + j, :], in_=bk[:, 0:256], func=AF.Copy), reads=[bkey], writes=["kmT"])
                else:
                    for m2 in range(2):
                        bk, bkey = nb()
                        for k in range(KC):
                            kb.op("pe", lambda e: e.matmul(bk[:], lhsT=memT[:, k, 128 * m2:128 * (m2 + 1)], rhs=w[:, k, :], start=(k == 0), stop=(k == KC - 1)),
                                  reads=["memT", wkey], writes=[bkey], sig=(k == KC - 1))
                        kb.op("dve", lambda e: e.tensor_copy(out=vmt[:, m2, 512 * (g - 2):512 * (g - 1)], in_=bk[:]), reads=[bkey], writes=["vmt"])
            for (t0, sg2) in tiles:
                if sg2 != seg:
                    continue
                kb.dma("sp", qmt[:], qmT.ap().rearrange("(c p) t -> p c t", p=128)[:, :, t0:t0 + TT], writes=["qmt"])
                kb.dma("sp", sgm[:], sgmem.ap().rearrange("(c p) t -> p c t", p=128)[:, :, t0:t0 + TT], writes=["sgm"])
                for h in range(4):
                    pk = []
                    for m2 in range(2):
                        bk, bkey = nb()
                        for c2 in range(2):
                            kb.op("pe", lambda e: e.matmul(bk[:, 0:TT], lhsT=kmT[:, 2 * h + c2, 128 * m2:128 * (m2 + 1)], rhs=qmt[:, 2 * h + c2, :], start=(c2 == 0), stop=(c2 == 1)),
                                  reads=["kmT", "qmt"], writes=[bkey], sig=(c2 == 1))
                        pi = npm % 4
                        npm += 1
                        kb.op("act", lambda e: e.activation(out=Pm[pi][:], in_=bk[:, 0:TT], func=AF.Exp), reads=[bkey], writes=[f"Pm{pi}"])
                        pk.append(pi)
                    lbk, lkey = nb()
                    for m2 in range(2):
                        kb.op("pe", lambda e: e.matmul(lbk[:, 0:TT], lhsT=ones_bf[:], rhs=Pm[pk[m2]][:], start=(m2 == 0), stop=(m2 == 1)),
                              reads=["ones", f"Pm{pk[m2]}"], writes=[lkey], sig=(m2 == 1))
                    kb.op("dve", lambda e: e.reciprocal(out=recm[:], in_=lbk[:, 0:TT]), reads=[lkey], writes=["recm"])
                    for c2 in range(2):
                        obk, okey2 = nb()
                        for m2 in range(2):
                            kb.op("pe", lambda e: e.matmul(obk[:, 0:TT], lhsT=vmt[:, m2, 256 * h + 128 * c2:256 * h + 128 * (c2 + 1)], rhs=Pm[pk[m2]][:], start=(m2 == 0), stop=(m2 == 1)),
                                  reads=["vmt", f"Pm{pk[m2]}"], writes=[okey2], sig=(m2 == 1))
                        kb.op("dve", lambda e: e.tensor_tensor(out=otm[:], in0=obk[:, 0:TT], in1=recm[:], op=ALU.mult), reads=[okey2, "recm"], writes=["otm"])
                        oi = nobm % 2
                        nobm += 1
                        kb.op("pool", lambda e: e.tensor_tensor(out=obm[oi][:], in0=otm[:], in1=sgm[:, 2 * h + c2, :], op=ALU.mult), reads=["otm", "sgm"], writes=[f"obm{oi}"])
                        r0 = 3072 + 256 * h + 128 * c2
                        kb.dma("sp", mixT.ap()[r0:r0 + 128, t0:t0 + TT], obm[oi][:], reads=[f"obm{oi}"])
        kb.end_stage()


    if not zero_rest:
        I32 = mybir.dt.int32
        KP = int(math.log2(NP))
        KS = int(math.log2(NS))
        KMAX = max(KP, KS)
        NQ = KMAX + 1
        PAD = max(NP, NS) // 2
        offs = [PAD, PAD + NP + PAD]
        lens = [NP, NS]
        LA = PAD + NP + PAD + NS + PAD
        WGd = dscr("WGd", [32, 8 * 64 * 2 * 128])
        KKd = dscr("KKd", [32, 32 * 15 * 32])
        RDd = dscr("RDd", [128, 2 * 32 * 8 * 2 * 32])
        DDd = dscr("DDd", [32, 32 * 32])
        Qd = dscr("Qd", [128, 3 * 64 * NQ], F32)
        Ep_own = dscr("Ep_own", [128, 128], F32)
        Es_own = dscr("Es_own", [128, 128], F32)
        Ep_all = dscr("Ep_all", [8 * 128, 128], F32)
        Es_all = dscr("Es_all", [4 * 128, 128], F32)
        XINd = dscr("XINd", [128, 256], F32)
        yT = dscr("yT", [1024, NT])

        kb.begin_stage()
        identf = kb.tile("identf", [128, 128], F32)
        kb.dma("sp", identf[:], ident_f_in.ap(), writes=["identf"])
        pb_ = [kb.psum(f"cp{i}", [128, 512], F32) for i in range(4)]
        ncp = [0]

        def nbk():
            i = ncp[0] % 4
            ncp[0] += 1
            return pb_[i], f"P_cp{i}"

        uid = [0]

        def T64(name):
            return kb.tile(name, [128, 64], F32)

        def tt(out, a, b, op, eng="dve"):
            uid[0] += 1
            kb.op(eng, lambda e: e.tensor_tensor(out=out, in0=a, in1=b, op=op), reads=["c0"], writes=["c0"])

        def ts(out, a, sc, op, eng="dve"):
            kb.op(eng, lambda e: e.tensor_scalar(out=out, in0=a, scalar1=sc, scalar2=None, op0=op), reads=["c0"], writes=["c0"])

        def actf(out, a, fn, scale=1.0):
            kb.op("act", lambda e: e.activation(out=out, in_=a, func=fn, scale=scale), reads=["c0"], writes=["c0"])

        LRn = kb.tile("LRn", [64, 128], F32)
        LIn = kb.tile("LIn", [64, 128], F32)
        LSb = kb.tile("LSb", [128, 128], F32)
        kb.dma("sp", LRn[:], lam_re.ap(), writes=["c0"])
        kb.dma("sp", LIn[:], lam_im.ap(), writes=["c0"])
        kb.dma("sp", LSb[:], dram_bcast(log_step, 128, 128), writes=["c0"])
        LR, LI, DT, MAG, SN, CS, AR, AI, T1, T2, T3, FR, FI = [T64(n) for n in ["LR", "LI", "DT", "MAG", "SN", "CS", "AR", "AI", "T1", "T2", "T3", "FR", "FI"]]
        KI = kb.tile("KI", [128, 64], I32)
        for src, dst in [(LRn, LR), (LIn, LI)]:
            bk, bkey = nbk()
            kb.op("pe", lambda e: e.transpose(out=bk[:, 0:64], in_=src[:], identity=identf[0:64, 0:64]), reads=["c0", "identf"], writes=[bkey, "c0"])
            kb.op("dve", lambda e: e.tensor_copy(out=dst[:], in_=bk[:, 0:64]), reads=[bkey, "c0"], writes=["c0"])
        lsv = LSb[:].rearrange("p (j g) -> p j g", g=2)
        actf(DT[0:64, :], lsv[0:64, :, 0], AF.Exp)
        actf(DT[64:128, :], lsv[64:128, :, 1], AF.Exp)
        tt(T1[:], LR[:], DT[:], ALU.mult)
        actf(MAG[:], T1[:], AF.Exp)
        tt(T1[:], LI[:], DT[:], ALU.mult)
        ts(T1[:], T1[:], 1.0 / TWO_PI, ALU.mult)

        def trig64(dst, shift):
            ts(T2[:], T1[:], shift, ALU.add)
            kb.op("dve", lambda e: e.tensor_copy(out=KI[:], in_=T2[:]), reads=["c0"], writes=["c0"])
            kb.op("dve", lambda e: e.tensor_copy(out=T3[:], in_=KI[:]), reads=["c0"], writes=["c0"])
            tt(T2[:], T2[:], T3[:], ALU.subtract)
            kb.op("dve", lambda e: e.scalar_tensor_tensor(out=T3[:], in0=T2[:], scalar=0.5, in1=T2[:], op0=ALU.is_gt, op1=ALU.subtract), reads=["c0"], writes=["c0"])
            kb.op("dve", lambda e: e.scalar_tensor_tensor(out=T2[:], in0=T3[:], scalar=0.5, in1=T3[:], op0=ALU.is_gt, op1=ALU.subtract), reads=["c0"], writes=["c0"])
            actf(dst[:], T2[:], AF.Sin, scale=TWO_PI)

        trig64(SN, 0.0)
        trig64(CS, 0.25)
        tt(AR[:], MAG[:], CS[:], ALU.mult)
        tt(AI[:], MAG[:], SN[:], ALU.mult)
        tt(T1[:], LR[:], LR[:], ALU.mult)
        tt(T2[:], LI[:], LI[:], ALU.mult)
        tt(T1[:], T1[:], T2[:], ALU.add)
        kb.op("dve", lambda e: e.reciprocal(out=T1[:], in_=T1[:]), reads=["c0"], writes=["c0"])
        ts(T2[:], AR[:], -1.0, ALU.add)
        tt(FR[:], T2[:], LR[:], ALU.mult)
        tt(T3[:], AI[:], LI[:], ALU.mult)
        tt(FR[:], FR[:], T3[:], ALU.add)
        tt(FR[:], FR[:], T1[:], ALU.mult)
        tt(FI[:], AI[:], LR[:], ALU.mult)
        tt(T3[:], T2[:], LI[:], ALU.mult)
        tt(FI[:], FI[:], T3[:], ALU.subtract)
        tt(FI[:], FI[:], T1[:], ALU.mult)
        Qall = kb.tile("Qall", [128, 3, 64, NQ], F32)
        kb.op("dve", lambda e: e.tensor_copy(out=Qall[:, 0, :, 0], in_=AR[:]), reads=["c0"], writes=["c0"])
        kb.op("dve", lambda e: e.tensor_copy(out=Qall[:, 1, :, 0], in_=AI[:]), reads=["c0"], writes=["c0"])
        for k in range(1, NQ):
            tt(T1[:], Qall[:, 0, :, k - 1], Qall[:, 0, :, k - 1], ALU.mult)
            tt(T2[:], Qall[:, 1, :, k - 1], Qall[:, 1, :, k - 1], ALU.mult)
            tt(Qall[:, 0, :, k], T1[:], T2[:], ALU.subtract)
            tt(T3[:], Qall[:, 0, :, k - 1], Qall[:, 1, :, k - 1], ALU.mult)
            ts(Qall[:, 1, :, k], T3[:], 2.0, ALU.mult)
        ts(Qall[:, 2, :, :], Qall[:, 1, :, :], -1.0, ALU.mult)
        kb.dma("sp", Qd.ap(), Qall[:].rearrange("p a j k -> p (a j k)"), reads=["c0"])
        BR = kb.tile("BR", [128, 64, 16], F32)
        BI = kb.tile("BI", [128, 64, 16], F32)
        BBR = kb.tile("BBR", [128, 64, 16], F32)
        BBI = kb.tile("BBI", [128, 64, 16], F32)
        T16 = kb.tile("T16", [128, 64, 16], F32)
        PBR = kb.tile("PBR", [128, 64, 16], F32)
        PBI = kb.tile("PBI", [128, 64, 16], F32)
        for g2 in range(2):
            for src, dst in [(b_re, BR), (b_im, BI)]:
                sv = src.ap().rearrange("d (gp g2) p c -> g2 d p gp c", g2=2)
                for d_ in range(2):
                    for q0 in range(0, 32, 8):
                        kb.dma("sp", dst[64 * g2:64 * (g2 + 1), 32 * d_ + q0:32 * d_ + q0 + 8, :], sv[g2, d_, :, q0:q0 + 8, :], writes=["c0"])

        def bc16(ap64):
            return ap64.unsqueeze(2).to_broadcast([128, 64, 16])

        def cmul16(outr, outi, ar_, ai_, br_, bi_):
            tt(outr, br_, bc16(ar_), ALU.mult)
            tt(T16[:], bi_, bc16(ai_), ALU.mult)
            tt(outr, outr, T16[:], ALU.subtract)
            tt(outi, bi_, bc16(ar_), ALU.mult)
            tt(T16[:], br_, bc16(ai_), ALU.mult)
            tt(outi, outi, T16[:], ALU.add)

        cmul16(BBR[:], BBI[:], FR[:], FI[:], BR[:], BI[:])
        CXf = kb.tile("CXf", [128, 2, 32, 32], F32)
        CXin = kb.tile("CXin", [32, 32, 128], F32)
        for part, srcC in enumerate([c_re, c_im]):
            kb.op("pool", lambda e: e.memset(CXin[:], 0.0), reads=["c0"], writes=["c0"])
            for g2 in range(2):
                kb.dma("sp", CXin[16 * g2:16 * (g2 + 1), :, 64 * g2:64 * (g2 + 1)],
                       srcC.ap().rearrange("(gp g2) co p -> g2 co gp p", g2=2)[g2], reads=["c0"], writes=["c0"])
            for g0 in range(0, 32, 16):
                bk, bkey = nbk()
                for gg in range(16):
                    kb.op("pe", lambda e: e.transpose(out=bk[:, 32 * gg:32 * (gg + 1)], in_=CXin[:, g0 + gg, :], identity=identf[0:32, 0:32]),
                          reads=["c0", "identf"], writes=[bkey], sig=(gg == 15))
                kb.op("act", lambda e: e.activation(out=CXf[:, part, g0:g0 + 16, :], in_=bk[:, :].rearrange("p (g c) -> p g c", g=16), func=AF.Copy, scale=(1.0 if part == 0 else -1.0)),
                      reads=[bkey, "c0"], writes=["c0"])
        PK = kb.tile("PK", [128, 2, 9, 64], F32)
        kb.op("pool", lambda e: e.memset(PK[:, 0, 0, :], 1.0), reads=["c0"], writes=["c0"])
        kb.op("pool", lambda e: e.memset(PK[:, 1, 0, :], 0.0), reads=["c0"], writes=["c0"])
        for k in range(1, 9):
            tt(T1[:], PK[:, 0, k - 1, :], AR[:], ALU.mult)
            tt(T2[:], PK[:, 1, k - 1, :], AI[:], ALU.mult)
            tt(PK[:, 0, k, :], T1[:], T2[:], ALU.subtract)
            tt(T1[:], PK[:, 0, k - 1, :], AI[:], ALU.mult)
            tt(T2[:], PK[:, 1, k - 1, :], AR[:], ALU.mult)
            tt(PK[:, 1, k, :], T1[:], T2[:], ALU.add)
        EXPT = [kb.tile(f"EXPT{i}", [128, 64, 32], F32) for i in range(2)]
        WGsb = kb.tile("WGsb", [32, 64, 2, 128], BF16)
        KKsb = kb.tile("KKsb", [32, 32, 15, 32], BF16)
        for part in range(2):
            kb.op("pool", lambda e: e.memset(EXPT[part][:], 0.0), reads=["c0"], writes=["c0"])
        WGd_v = WGd.ap().rearrange("p (k x) -> p k x", k=8)
        for k in range(8):
            cmul16(PBR[:], PBI[:], PK[:, 0, k, :], PK[:, 1, k, :], BBR[:], BBI[:])
            for part, srcB in enumerate([PBR, PBI]):
                kb.op("dve", lambda e: e.tensor_copy(out=EXPT[part][0:64, :, 0:16], in_=srcB[0:64, :, :]), reads=["c0"], writes=["c0"])
                kb.op("dve", lambda e: e.tensor_copy(out=EXPT[part][64:128, :, 16:32], in_=srcB[64:128, :, :]), reads=["c0"], writes=["c0"])
            for part in range(2):
                for j0 in range(0, 64, 4):
                    bk, bkey = nbk()
                    for jj in range(4):
                        kb.op("pe", lambda e: e.transpose(out=bk[0:32, 128 * jj:128 * (jj + 1)], in_=EXPT[part][:, j0 + jj, :], identity=identf[:]),
                              reads=["c0", "identf"], writes=[bkey], sig=(jj == 3))
                    kb.op("act", lambda e: e.activation(out=WGsb[:, j0:j0 + 4, part, :], in_=bk[0:32, :].rearrange("p (j c) -> p j c", j=4), func=AF.Copy), reads=[bkey, "wgd"], writes=["wg"])
            kb.dma("sp", WGd_v[:, k, :], WGsb[:].rearrange("p j a c -> p (j a c)"), reads=["wg"], writes=["wgd"])
            if k == 0:
                for g0 in range(0, 32, 16):
                    bk, bkey = nbk()
                    for gg in range(16):
                        gp_ = g0 + gg
                        ops_ = [(EXPT[0][:, gp_, :], CXf[:, 0, gp_, :]), (EXPT[1][:, gp_, :], CXf[:, 1, gp_, :]),
                                (EXPT[0][:, 32 + gp_, :], CXf[:, 0, gp_, :]), (EXPT[1][:, 32 + gp_, :], CXf[:, 1, gp_, :])]
                        for i_, (l_, r_) in enumerate(ops_):
                            kb.op("pe", lambda e: e.matmul(bk[0:32, 32 * gg:32 * (gg + 1)], lhsT=l_, rhs=r_, start=(i_ == 0), stop=(i_ == 3)),
                                  reads=["c0"], writes=[bkey], sig=(gg == 15 and i_ == 3))
                    kb.op("act", lambda e: e.activation(out=KKsb[:, g0:g0 + 16, 7, :], in_=bk[0:32, :].rearrange("p (g c) -> p g c", g=16), func=AF.Copy), reads=[bkey], writes=["kk"])
            else:
                for j0 in range(0, 64, 16):
                    d_ = j0 // 32
                    g0 = j0 % 32
                    idx = 7 + k if d_ == 0 else 7 - k
                    bk, bkey = nbk()
                    for gg in range(16):
                        j_ = j0 + gg
                        gp_ = g0 + gg
                        kb.op("pe", lambda e: e.matmul(bk[0:32, 32 * gg:32 * (gg + 1)], lhsT=EXPT[0][:, j_, :], rhs=CXf[:, 0, gp_, :], start=True, stop=False),
                              reads=["c0"], writes=[bkey], sig=False)
                        kb.op("pe", lambda e: e.matmul(bk[0:32, 32 * gg:32 * (gg + 1)], lhsT=EXPT[1][:, j_, :], rhs=CXf[:, 1, gp_, :], start=False, stop=True),
                              reads=["c0"], writes=[bkey], sig=(gg == 15))
                    kb.op("act", lambda e: e.activation(out=KKsb[:, g0:g0 + 16, idx, :], in_=bk[0:32, :].rearrange("p (g c) -> p g c", g=16), func=AF.Copy), reads=[bkey], writes=["kk"])
        kb.dma("sp", KKd.ap(), KKsb[:].rearrange("p g i c -> p (g i c)"), reads=["kk"])
        TA = kb.tile("TA", [128, 32, 32], F32)
        TB = kb.tile("TB", [128, 32, 32], F32)
        RDh = kb.tile("RDh", [128, 32, 8, 2, 32], BF16)
        RDd_v = RDd.ap().rearrange("p (d x) -> p d x", d=2)
        for d_ in range(2):
            for k in range(1, 9):
                pre = PK[:, 0, k, 32 * d_:32 * (d_ + 1)].unsqueeze(2).to_broadcast([128, 32, 32])
                pim = PK[:, 1, k, 32 * d_:32 * (d_ + 1)].unsqueeze(2).to_broadcast([128, 32, 32])
                tt(TA[:], CXf[:, 0, :, :], pre, ALU.mult)
                tt(TB[:], CXf[:, 1, :, :], pim, ALU.mult)
                kb.op("dve", lambda e: e.tensor_tensor(out=RDh[:, :, k - 1, 0, :], in0=TA[:], in1=TB[:], op=ALU.add), reads=["c0", "rdd"], writes=["c0", "rdh"])
                tt(TA[:], CXf[:, 1, :, :], pre, ALU.mult)
                tt(TB[:], CXf[:, 0, :, :], pim, ALU.mult)
                kb.op("dve", lambda e: e.tensor_tensor(out=RDh[:, :, k - 1, 1, :], in0=TA[:], in1=TB[:], op=ALU.subtract), reads=["c0", "rdd"], writes=["c0", "rdh"])
            kb.dma("sp", RDd_v[:, d_, :], RDh[:].rearrange("p g k a c -> p (g k a c)"), reads=["rdh"], writes=["rdd"])
        Dn = kb.tile("Dn", [32, 32], F32)
        Dt_ = kb.tile("Dt_", [32, 32], F32)
        DDsb = kb.tile("DDsb", [32, 32, 32], BF16)
        kb.dma("sp", Dn[:], ssm_d.ap(), writes=["dn"])
        bk, bkey = nbk()
        kb.op("pe", lambda e: e.transpose(out=bk[0:32, 0:32], in_=Dn[:], identity=identf[0:32, 0:32]), reads=["dn", "identf"], writes=[bkey])
        kb.op("dve", lambda e: e.tensor_copy(out=Dt_[:], in_=bk[0:32, 0:32]), reads=[bkey], writes=["dt_"])
        for gp in range(32):
            kb.op("dve", lambda e: e.tensor_scalar(out=DDsb[:, gp, :], in0=identf[0:32, 0:32], scalar1=Dt_[:, gp:gp + 1], scalar2=None, op0=ALU.mult),
                  reads=["dt_", "identf"], writes=["dd"])
        kb.dma("sp", DDd.ap(), DDsb[:].rearrange("p g c -> p (g c)"), reads=["dd"])
        kb.end_stage()

        if stop == "C0":
            kb.stack.close()
            return nc
        def ssm_pass(final):
            kb.begin_stage()
            NMP, NMS = NP // 8, NS // 8
            NM = NMP + NMS
            DL = max(NMP, NMS)
            PADm = DL // 2
            blk = DL + PADm
            moffs = [PADm, PADm + blk]
            mlens = [NMP, NMS]
            mbase = [0, NMP]
            LAm = PADm + 2 * blk + PADm

            def bv(t_, start):
                return t_[:, start:start + 2 * blk].rearrange("p (b x) -> p b x", b=2)[:, :, 0:DL]
            KM = int(math.log2(max(NMP, NMS)))
            DD = kb.tile("DD", [32, 32, 32], BF16)
            Q = kb.tile("Q", [128, 3, 64, NQ], F32)
            kb.dma("sp", DD[:].rearrange("p g c -> p (g c)"), DDd.ap(), writes=["DD"])
            kb.dma("sp", Q[:].rearrange("p a j k -> p (a j k)"), Qd.ap(), writes=["Q"])
            XIN = kb.tile("XIN", [128, 2, 2, 2, 32], F32)
            Eloc = kb.tile("Eloc", [128, 2, 2, 2, 32], F32)
            if final:
                kb.dma("sp", XIN[:].rearrange("p s d a g -> p (s d a g)"), XINd.ap(), writes=["XIN"])
            XA = [[kb.tile(f"XA{d}{i}", [128, LAm], F32) for i in range(2)] for d in range(2)]
            XB = [[kb.tile(f"XB{d}{i}", [128, LAm], F32) for i in range(2)] for d in range(2)]
            for d in range(2):
                for i in range(2):
                    kb.op("pool", lambda e: e.memset(XA[d][i][:], 0.0), writes=[f"xa{d}{i}"])
                    kb.op("pool", lambda e: e.memset(XB[d][i][:], 0.0), writes=[f"xb{d}{i}"])
            Xbf = kb.tile("Xbf", [128, 2, 2, NM], BF16)
            U32 = [kb.tile(f"U32{i}", [32, NT], BF16) for i in range(2)]
            Y32 = [kb.tile(f"Y32{i}", [32, NT], BF16) for i in range(2)]
            WGp = [kb.tile(f"WGp{i}", [32, 8, 2, 2, 128], BF16) for i in range(2)]
            RDp = [kb.tile(f"RDp{i}", [128, 2, 8, 2, 32], BF16) for i in range(2)]
            KKp = [kb.tile(f"KKp{i}", [32, 15, 32], BF16) for i in range(2)]
            zps = [kb.psum(f"zps{i}", [128, 512], F32) for i in range(4)]
            yps = [kb.psum(f"yps{i}", [128, 512], F32) for i in range(2)]
            nz = 0
            ny = 0
            WGd_v = WGd.ap().rearrange("p (k d g a c) -> p k d g a c", k=8, d=2, g=32, a=2)
            RDd_v = RDd.ap().rearrange("p (d g x) -> p d g x", d=2, g=32)
            KKd_v = KKd.ap().rearrange("p (g x) -> p g x", g=32)
            for gp in range(32):
                b2 = gp % 2
                u = U32[b2]
                ukey = f"U32{b2}"
                kb.dma("sp", u[:], uT.ap()[32 * gp:32 * (gp + 1), :], writes=[ukey])
                kb.dma("sp", WGp[b2][:], WGd_v[:, :, :, gp, :, :], writes=[f"WGp{b2}"])
                if final:
                    kb.dma("sp", RDp[b2][:].rearrange("p d k a c -> p d (k a c)"), RDd_v[:, :, gp, :], writes=[f"RDp{b2}"])
                    kb.dma("sp", KKp[b2][:].rearrange("p i c -> p (i c)"), KKd_v[:, gp, :], writes=[f"KKp{b2}"])
                uv = u[:].rearrange("p (m t) -> p m t", t=8)
                for d in range(2):
                    j = 32 * d + gp
                    for part in range(2):
                        zi = nz % 4
                        nz += 1
                        for tp in range(8):
                            k_ = 7 - tp if d == 0 else tp
                            kb.op("pe", lambda e: e.matmul(zps[zi][:, 0:NM], lhsT=WGp[b2][:, k_, d, part, :], rhs=uv[:, :, tp], start=(tp == 0), stop=(tp == 7)),
                                  reads=[f"WGp{b2}", ukey], writes=[f"P_zps{zi}"], sig=(tp == 7))
                        for sg in range(2):
                            kb.op("act", lambda e: e.activation(out=XA[d][part][:, moffs[sg]:moffs[sg] + mlens[sg]], in_=zps[zi][:, mbase[sg]:mbase[sg] + mlens[sg]], func=AF.Copy),
                                  reads=[f"P_zps{zi}"], writes=[f"xa{d}{part}"])
                    if final:
                        for sg in range(2):
                            slot = moffs[sg] - 1 if d == 0 else moffs[sg] + mlens[sg]
                            for part in range(2):
                                kb.op("pool", lambda e: e.tensor_copy(out=XA[d][part][:, slot:slot + 1], in_=XIN[:, sg, d, part, gp:gp + 1]), reads=["XIN"], writes=[f"xa{d}{part}"])
                                kb.op("pool", lambda e: e.tensor_copy(out=XB[d][part][:, slot:slot + 1], in_=XIN[:, sg, d, part, gp:gp + 1]), reads=["XIN"], writes=[f"xb{d}{part}"])
                    src, dst, sk, dk = XA[d], XB[d], f"xa{d}", f"xb{d}"
                    for k in range(KM):
                        sh = 1 << k
                        sgn = -sh if d == 0 else sh
                        qr = Q[:, 0, j, k + 3:k + 4]
                        qi = Q[:, 1, j, k + 3:k + 4]
                        nqi = Q[:, 2, j, k + 3:k + 4]
                        A_ = moffs[0]
                        B_ = moffs[0] + sgn
                        kb.op("dve", lambda e: e.scalar_tensor_tensor(out=bv(dst[0], A_), in0=bv(src[0], B_), scalar=qr, in1=bv(src[0], A_), op0=ALU.mult, op1=ALU.add),
                              reads=[sk + "0", sk + "1", "Q"], writes=[dk + "0"])
                        kb.op("dve", lambda e: e.scalar_tensor_tensor(out=bv(dst[0], A_), in0=bv(src[1], B_), scalar=nqi, in1=bv(dst[0], A_), op0=ALU.mult, op1=ALU.add),
                              reads=[sk + "0", sk + "1", "Q"], writes=[dk + "0"])
                        kb.op("dve", lambda e: e.scalar_tensor_tensor(out=bv(dst[1], A_), in0=bv(src[1], B_), scalar=qr, in1=bv(src[1], A_), op0=ALU.mult, op1=ALU.add),
                              reads=[sk + "0", sk + "1", "Q"], writes=[dk + "1"])
                        kb.op("dve", lambda e: e.scalar_tensor_tensor(out=bv(dst[1], A_), in0=bv(src[0], B_), scalar=qi, in1=bv(dst[1], A_), op0=ALU.mult, op1=ALU.add),
                              reads=[sk + "0", sk + "1", "Q"], writes=[dk + "1"])
                        src, dst, sk, dk = dst, src, dk, sk
                    if not final:
                        for sg in range(2):
                            p0 = moffs[sg] + mlens[sg] - 1 if d == 0 else moffs[sg]
                            for part in range(2):
                                kb.op("act", lambda e: e.activation(out=Eloc[:, sg, d, part, gp:gp + 1], in_=src[part][:, p0:p0 + 1], func=AF.Copy), reads=[sk + str(part)], writes=["Eloc"])
                    else:
                        shf = -1 if d == 0 else 1
                        for part in range(2):
                            for sg in range(2):
                                o, n = moffs[sg], mlens[sg]
                                kb.op("act", lambda e: e.activation(out=Xbf[:, d, part, mbase[sg]:mbase[sg] + n], in_=src[part][:, o + shf:o + shf + n], func=AF.Copy),
                                      reads=[sk + str(part)], writes=[f"Xbf{d}"])
                if final:
                    y = Y32[b2]
                    ykey = f"Y32{b2}"
                    yv = y[:].rearrange("p (m t) -> p m t", t=8)
                    for tau in range(8):
                        yi = ny % 2
                        ny += 1
                        mm = [(KKp[b2][:, tau - tp + 7, :], uv[:, :, tp]) for tp in range(8)]
                        mm += [(RDp[b2][:, 0, tau, pt, :], Xbf[:, 0, pt, :]) for pt in range(2)]
                        mm += [(RDp[b2][:, 1, 7 - tau, pt, :], Xbf[:, 1, pt, :]) for pt in range(2)]
                        mm += [(DD[:, gp, :], uv[:, :, tau])]
                        for i_, (l_, r_) in enumerate(mm):
                            kb.op("pe", lambda e: e.matmul(yps[yi][0:32, 0:NM], lhsT=l_, rhs=r_, start=(i_ == 0), stop=(i_ == len(mm) - 1)),
                                  reads=[f"KKp{b2}", f"RDp{b2}", "DD", "Xbf0", "Xbf1", ukey], writes=[f"P_yps{yi}"], sig=(i_ == len(mm) - 1))
                        kb.op("act", lambda e: e.activation(out=yv[:, :, tau], in_=yps[yi][0:32, 0:NM], func=AF.Gelu), reads=[f"P_yps{yi}"], writes=[ykey])
                    kb.dma("sp", yT.ap()[32 * gp:32 * (gp + 1), :], y[:], reads=[ykey])
            if not final:
                kb.dma("sp", Ep_own.ap(), Eloc[:, 0, :, :, :].rearrange("p d a g -> p (d a g)"), reads=["Eloc"])
                kb.dma("sp", Es_own.ap(), Eloc[:, 1, :, :, :].rearrange("p d a g -> p (d a g)"), reads=["Eloc"])
            kb.end_stage()

        ssm_pass(False)
        if stop == "C1":
            kb.stack.close()
            return nc
        kb.begin_stage()
        kb.allgather(Ep_own.ap(), Ep_all.ap(), ALL8)
        kb.allgather(Es_own.ap(), Es_all.ap(), GRP4)
        kb.end_stage()

        kb.begin_stage()
        Q = kb.tile("Q", [128, 3, 64, NQ], F32)
        kb.dma("sp", Q[:].rearrange("p a j k -> p (a j k)"), Qd.ap(), writes=["cc"])
        EP = kb.tile("EP", [128, 8, 2, 2, 32], F32)
        ES = kb.tile("ES", [128, 4, 2, 2, 32], F32)
        kb.dma("sp", EP[:].rearrange("p r d a g -> p r (d a g)"), Ep_all.ap().rearrange("(r p) c -> p r c", p=128), writes=["cc"])
        kb.dma("sp", ES[:].rearrange("p r d a g -> p r (d a g)"), Es_all.ap().rearrange("(r p) c -> p r c", p=128), writes=["cc"])
        SELP = kb.tile("SELP", [128, 16], F32)
        SELS = kb.tile("SELS", [128, 8], F32)
        kb.dma("sp", SELP[:], dram_bcast(selp, 128, 16), writes=["cc"])
        kb.dma("sp", SELS[:], dram_bcast(sels, 128, 8), writes=["cc"])
        XINo = kb.tile("XINo", [128, 2, 2, 2, 32], F32)
        kb.op("pool", lambda e: e.memset(XINo[:], 0.0), reads=["cc"], writes=["cc"])
        cr, ci, n1, n2, n3 = [kb.tile(n_, [128, 32], F32) for n_ in ["cr", "ci", "n1", "n2", "n3"]]

        def cop(fn, eng="dve"):
            kb.op(eng, fn, reads=["cc"], writes=["cc"])

        for sg, (Et, nr_, SEL, kk) in enumerate([(EP, 8, SELP, KP), (ES, 4, SELS, KS)]):
            for d in range(2):
                anr = Q[:, 0, 32 * d:32 * (d + 1), kk]
                ani = Q[:, 1, 32 * d:32 * (d + 1), kk]
                cop(lambda e: e.memset(cr[:], 0.0))
                cop(lambda e: e.memset(ci[:], 0.0))
                order = range(nr_) if d == 0 else range(nr_ - 1, -1, -1)
                for r in order:
                    er = Et[:, r, d, 0, :]
                    ei = Et[:, r, d, 1, :]
                    cop(lambda e: e.tensor_tensor(out=n1[:], in0=cr[:], in1=anr, op=ALU.mult))
                    cop(lambda e: e.tensor_tensor(out=n2[:], in0=ci[:], in1=ani, op=ALU.mult))
                    cop(lambda e: e.tensor_tensor(out=n1[:], in0=n1[:], in1=n2[:], op=ALU.subtract))
                    cop(lambda e: e.tensor_tensor(out=n2[:], in0=cr[:], in1=ani, op=ALU.mult))
                    cop(lambda e: e.tensor_tensor(out=n3[:], in0=ci[:], in1=anr, op=ALU.mult))
                    cop(lambda e: e.tensor_tensor(out=ci[:], in0=n2[:], in1=n3[:], op=ALU.add))
                    cop(lambda e: e.tensor_tensor(out=ci[:], in0=ci[:], in1=ei, op=ALU.add))
                    cop(lambda e: e.tensor_tensor(out=cr[:], in0=n1[:], in1=er, op=ALU.add))
                    sc = SEL[:, nr_ * d + r:nr_ * d + r + 1]
                    cop(lambda e: e.scalar_tensor_tensor(out=XINo[:, sg, d, 0, :], in0=cr[:], scalar=sc, in1=XINo[:, sg, d, 0, :], op0=ALU.mult, op1=ALU.add))
                    cop(lambda e: e.scalar_tensor_tensor(out=XINo[:, sg, d, 1, :], in0=ci[:], scalar=sc, in1=XINo[:, sg, d, 1, :], op0=ALU.mult, op1=ALU.add))
        kb.dma("sp", XINd.ap(), XINo[:].rearrange("p s d a g -> p (s d a g)"), reads=["cc"])
        kb.end_stage()

        if stop == "C2":
            kb.stack.close()
            return nc
        ssm_pass(True)
        if stop == "C3":
            kb.stack.close()
            return nc

        kb.begin_stage()
        wgl = kb.tile("wgl", [128, 8, 1024], BF16)
        bgl = kb.tile("bgl", [128, 8], F32)
        kb.dma("sp", wgl[:], w_glu_bf.ap().rearrange("(k p) c -> p k c", p=128), writes=["wgl"])
        for c_ in range(8):
            kb.dma("sp", bgl[:, c_:c_ + 1], b_glu.ap()[c_:c_ + 1, :].rearrange("o p -> p o"), writes=["bgl"])
        yt = kb.tile("yt", [128, 8, TT], BF16)
        sgs = kb.tile("sgs", [128, 8, TT], BF16)
        sig_ = [kb.tile(f"sig{i}", [128, TT], F32) for i in range(2)]
        og = [kb.tile(f"og{i}", [128, TT], BF16) for i in range(2)]
        gps = [kb.psum(f"gps{i}", [128, 512], F32) for i in range(2)]
        ng = 0
        for (t0, seg) in tiles:
            kb.dma("sp", yt[:], yT.ap().rearrange("(c p) t -> p c t", p=128)[:, :, t0:t0 + TT], writes=["yt"])
            kb.dma("sp", sgs[:], sgssm.ap().rearrange("(c p) t -> p c t", p=128)[:, :, t0:t0 + TT], writes=["sgs"])
            for c in range(8):
                gi = ng % 2
                ng += 1
                for k in range(8):
                    kb.op("pe", lambda e: e.matmul(gps[gi][:, 0:TT], lhsT=wgl[:, k, 128 * c:128 * (c + 1)], rhs=yt[:, k, :], start=(k == 0), stop=(k == 7)),
                          reads=["wgl", "yt"], writes=[f"P_gps{gi}"], sig=(k == 7))
                kb.op("act", lambda e: e.activation(out=sig_[gi][:], in_=gps[gi][:, 0:TT], func=AF.Sigmoid, bias=bgl[:, c:c + 1]), reads=[f"P_gps{gi}", "bgl"], writes=[f"sig{gi}"])
                kb.op("dve", lambda e: e.tensor_tensor(out=sig_[gi][:], in0=sig_[gi][:], in1=yt[:, c, :], op=ALU.mult), reads=[f"sig{gi}", "yt"], writes=[f"sig{gi}"])
                kb.op("pool", lambda e: e.tensor_tensor(out=og[gi][:], in0=sig_[gi][:], in1=sgs[:, c, :], op=ALU.mult), reads=[f"sig{gi}", "sgs"], writes=[f"og{gi}"])
                kb.dma("sp", mixT.ap()[2048 + 128 * c:2048 + 128 * (c + 1), t0:t0 + TT], og[gi][:], reads=[f"og{gi}"])
        kb.end_stage()

    if zero_rest:
        kb.begin_stage()
        zt = kb.tile("zt", [128, NT], BF16)
        kb.op("pool", lambda e: e.memset(zt[:], 0.0), writes=["zt"])
        for r0 in range(2048, 4096 if zero_rest == True else 3072, 128):
            kb.dma("sp", mixT.ap()[r0:r0 + 128, :], zt[:], reads=["zt"])
        kb.end_stage()

    if debug:
        kb.begin_stage()
        for nm, (dst, src) in dbg.items():
            kb.dma("sp", dst.ap(), src.ap())
        kb.end_stage()

    kb.begin_stage()
    ALPHA = 2.0 ** 0.25
    CW = 256
    mt = kb.tile("mt", [128, 32, TT], BF16)
    wo = [kb.tile(f"wo{i}", [128, 32, CW], BF16) for i in range(2)]
    hh = [kb.tile(f"hh{s}", [128, D], F32) for s in range(SUB)]
    lg = kb.tile("lg", [128, D], F32)
    lb = kb.tile("lb", [128, D], F32)
    NST = max(1, D // 512)
    stats = kb.tile("stats", [128, NST, 6], F32)
    mv = kb.tile("mv", [128, 2], F32)
    rstd = kb.tile("rstd", [128, 1], F32)
    epsl = kb.tile("epsl", [128, 1], F32)
    kb.op("pool", lambda e: e.memset(epsl[:], 1e-5), writes=["epsl"])
    hps = [kb.psum(f"hps{i}", [128, 512], F32) for i in range(4)]
    kb.dma("sp", lg[:], dram_bcast(ln_g, 128, D), writes=["lg"])
    kb.dma("sp", lb[:], dram_bcast(ln_b, 128, D), writes=["lb"])
    w_out_v = w_out_bf.ap().rearrange("(k p) c -> p k c", p=128)
    mix_v = mixT.ap().rearrange("(k p) t -> p k t", p=128)
    nwo = 0
    nh = 0
    for (t0, seg) in tiles:
        kb.dma("sp", mt[:], mix_v[:, :, t0:t0 + TT], writes=["mt"])
        for s in range(SUB):
            kb.dma("sp", hh[s][:], x_own.ap()[t0 + 128 * s:t0 + 128 * (s + 1), :], writes=[f"hh{s}"])
        for cg in range(D // CW):
            w = wo[nwo % 2]
            wkey = f"wo{nwo % 2}"
            nwo += 1
            kb.dma("sp", w[:], w_out_v[:, :, CW * cg:CW * (cg + 1)], writes=[wkey])
            for s in range(SUB):
                hp = hps[nh % len(hps)]
                hkey = f"P_hps{nh % len(hps)}"
                nh += 1
                for k in range(32):
                    kb.op("pe", lambda e, hp=hp, k=k, s=s, w=w: e.matmul(hp[:, 0:CW], lhsT=mt[:, k, 128 * s:128 * (s + 1)], rhs=w[:, k, :], start=(k == 0), stop=(k == 31)),
                          reads=["mt", wkey], writes=[hkey], sig=(k == 31))
                kb.op("dve", lambda e, hp=hp, s=s, cg=cg: e.scalar_tensor_tensor(out=hh[s][:, CW * cg:CW * (cg + 1)], in0=hh[s][:, CW * cg:CW * (cg + 1)], scalar=ALPHA, in1=hp[:, 0:CW], op0=ALU.mult, op1=ALU.add),
                      reads=[hkey, f"hh{s}"], writes=[f"hh{s}"])
        for s in range(SUB):
            h = hh[s]
            for c8 in range(NST):
                kb.op("dve", lambda e, h=h, c8=c8: e.bn_stats(out=stats[:, c8, :], in_=h[:, (D // NST) * c8:(D // NST) * (c8 + 1)]), reads=[f"hh{s}"], writes=["stats"])
            kb.op("dve", lambda e: e.bn_aggr(out=mv[:], in_=stats[:]), reads=["stats"], writes=["mv"])
            kb.op("act", lambda e: e.activation(out=rstd[:], in_=mv[:, 1:2], func=AF.Sqrt, bias=epsl[:, 0:1]), reads=["mv", "epsl"], writes=["rstd"])
            kb.op("dve", lambda e: e.reciprocal(out=rstd[:], in_=rstd[:]), reads=["rstd"], writes=["rstd"])
            kb.op("dve", lambda e, h=h: e.tensor_scalar(out=h[:], in0=h[:], scalar1=mv[:, 0:1], scalar2=rstd[:, 0:1], op0=ALU.subtract, op1=ALU.mult),
                  reads=[f"hh{s}", "mv", "rstd"], writes=[f"hh{s}"])
            kb.op("pool", lambda e, h=h: e.tensor_tensor(out=h[:], in0=h[:], in1=lg[:], op=ALU.mult), reads=[f"hh{s}", "lg"], writes=[f"hh{s}"])
            kb.op("pool", lambda e, h=h: e.tensor_tensor(out=h[:], in0=h[:], in1=lb[:], op=ALU.add), reads=[f"hh{s}", "lb"], writes=[f"hh{s}"])
            kb.dma("sp", y_out.ap()[t0 + 128 * s:t0 + 128 * (s + 1), :], h[:], reads=[f"hh{s}"])
    kb.end_stage()
    kb.stack.close()
    return nc


def host_consts(cfg, core):
    NP, NS, NT = cfg.NP, cfg.NS, cfg.NT
    tp = core * NP + np.arange(NP)
    ts = (core % 4) * NS + np.arange(NS)
    t = np.concatenate([tp, ts])
    pos_row = (t // 64).astype(np.float32)[None, :]
    pos_col = (t % 64).astype(np.float32)[None, :]
    inv = (10000.0 ** (-np.arange(0, 64, 2, dtype=np.float32) / 64.0)).astype(np.float32)
    inv_col = np.concatenate([inv, inv, inv, inv]).astype(np.float32)[:, None]
    rot = np.zeros((128, 128), np.float32)
    for s in range(2):
        for j in range(32):
            rot[s * 64 + 32 + j, s * 64 + j] = -1.0
            rot[s * 64 + j, s * 64 + 32 + j] = 1.0
    selp = np.zeros((1, 16), np.float32)
    if core - 1 >= 0:
        selp[0, core - 1] = 1.0
    if core + 1 < 8:
        selp[0, 8 + core + 1] = 1.0
    sels = np.zeros((1, 8), np.float32)
    cs = core % 4
    if cs - 1 >= 0:
        sels[0, cs - 1] = 1.0
    if cs + 1 < 4:
        sels[0, 4 + cs + 1] = 1.0
    return dict(pos_row=pos_row, pos_col=pos_col, inv_col=inv_col, rotm=rot.astype(NPBF),
                ident_bf=np.eye(128, dtype=np.float32).astype(NPBF), ident_f=np.eye(128, dtype=np.float32),
                selp=selp, sels=sels)


def make_in_maps(cfg, inputs):
    NP, NS = cfg.NP, cfg.NS
    f = lambda a: np.ascontiguousarray(np.asarray(a, dtype=np.float32))
    xp = f(inputs["x_prompt"])[0]
    xs = f(inputs["x_sample"])
    shared = dict(
        mem_p=f(inputs["mem_prompt"])[0], w_in=f(inputs["w_in"])[0], w_out=f(inputs["w_out"])[0],
        w_glu=f(inputs["w_glu"])[0], w_mkv=f(inputs["w_mem_kv"])[0],
        qg=f(inputs["q_norm_g"])[0][:, None], kg=f(inputs["k_norm_g"])[0][:, None],
        lam_re=f(inputs["ssm_lam_re"])[0].reshape(64, 128), lam_im=f(inputs["ssm_lam_im"])[0].reshape(64, 128),
        log_step=f(inputs["ssm_log_step"])[0].reshape(1, 128),
        b_re=f(inputs["ssm_b_re"])[0], b_im=f(inputs["ssm_b_im"])[0],
        c_re=f(inputs["ssm_c_re"])[0], c_im=f(inputs["ssm_c_im"])[0],
        ssm_d=f(inputs["ssm_d"])[0].reshape(32, 32), b_glu=f(inputs["b_glu"])[0].reshape(8, 128),
        ln_g=f(inputs["ln_g"]), ln_b=f(inputs["ln_b"]),
    )
    maps = []
    for c in range(NCORES):
        b = c // 4
        m = dict(shared)
        m["x_own"] = np.ascontiguousarray(np.concatenate([xp[c * NP:(c + 1) * NP], xs[b, (c % 4) * NS:(c % 4 + 1) * NS]], 0))
        m["mem_s"] = f(inputs["mem_sample"])[b]
        m.update(host_consts(cfg, c))
        maps.append(m)
    return maps


def assemble(cfg, results):
    NP, NS = cfg.NP, cfg.NS
    yp = np.concatenate([r["y_out"][:NP] for r in results], 0)[None]
    ys = np.stack([np.concatenate([results[4 * b + i]["y_out"][NP:] for i in range(4)], 0) for b in range(2)], 0)
    return yp.astype(np.float32), ys.astype(np.float32)


def kernel(**inputs):
    cfg = Cfg(inputs["x_prompt"].shape[1], inputs["x_sample"].shape[1], inputs["x_prompt"].shape[2])
    nc = build(cfg)
    maps = make_in_maps(cfg, inputs)
    res = run_bass_kernel_spmd(nc, maps, core_ids=list(range(NCORES)))
    return assemble(cfg, res.results)
```

```python
import math
from contextlib import ExitStack
import numpy as np
import ml_dtypes
import concourse.bass as bass
import concourse.mybir as mybir
from concourse.bass_utils import run_bass_kernel_spmd

F32 = mybir.dt.float32
BF16 = mybir.dt.bfloat16
AF = mybir.ActivationFunctionType
ALU = mybir.AluOpType
NPBF = ml_dtypes.bfloat16

INW = 9216
NCORES = 8
TWO_PI = 2.0 * math.pi


class Tok:
    __slots__ = ("sem", "val", "eng", "kind")

    def __init__(self, sem, val, eng, kind):
        self.sem, self.val, self.eng, self.kind = sem, val, eng, kind


class KB:
    def __init__(self, nc):
        self.nc = nc
        self.E = {"pe": nc.tensor, "act": nc.scalar, "dve": nc.vector, "pool": nc.gpsimd, "sp": nc.sync}
        self.stack = ExitStack()
        ec = self.stack.enter_context
        self.csem = {e: ec(nc.semaphore("cs_" + e)) for e in ["pe", "act", "dve", "pool"]}
        self.ccnt = {e: 0 for e in self.csem}
        self.dpool = {q: [ec(nc.semaphore(f"ds_{q}{i}")) for i in range(n)] for q, n in [("sp", 16), ("pool", 8), ("act", 4)]}
        self.dcnt = {q: [0] * len(v) for q, v in self.dpool.items()}
        self.dn = {q: 0 for q in self.dpool}
        self.ccsem = ec(nc.semaphore("cc_sem"))
        self.cccnt = 0
        self.waited = {}
        self.lastw = {}
        self.readers = {}
        self.pe_cur = Tok(self.csem["pe"], None, "pe", "c")
        self.pe_unsig = 0
        self.stage_stack = None
        self.uid = 0

    def tile(self, name, shape, dtype):
        self.uid += 1
        return self.stage_stack.enter_context(self.nc.sbuf_tensor(f"{name}_{self.uid}", list(shape), dtype))

    def psum(self, name, shape, dtype):
        self.uid += 1
        return self.stage_stack.enter_context(self.nc.psum_tensor(f"{name}_{self.uid}", list(shape), dtype))

    def begin_stage(self):
        self.stage_stack = ExitStack()

    def end_stage(self):
        self.barrier()
        self.stage_stack.close()
        self.stage_stack = None

    def _wait(self, eng, tok):
        if tok.val is None:
            raise RuntimeError("dependency on unsignaled PE op")
        key = (eng, tok.sem.name)
        if self.waited.get(key, 0) >= tok.val:
            return
        self.E[eng].wait_ge(tok.sem, tok.val)
        self.waited[key] = tok.val

    def _deps(self, eng, reads, writes, is_dma):
        for k in reads:
            t = self.lastw.get(k)
            if t is not None and not (not is_dma and t.kind == "c" and t.eng == "pe" and eng == "pe"):
                self._wait(eng, t)
        for k in writes:
            t = self.lastw.get(k)
            if t is not None and not (not is_dma and t.kind == "c" and t.eng == "pe" and eng == "pe"):
                self._wait(eng, t)
            for r in self.readers.get(k, {}).values():
                if (not is_dma) and r.kind == "c" and r.eng == eng:
                    continue
                self._wait(eng, r)

    def _record(self, tok, reads, writes):
        for k in reads:
            self.readers.setdefault(k, {})[tok.sem.name] = tok
        for k in writes:
            self.lastw[k] = tok
            self.readers[k] = {}

    def op(self, eng, fn, reads=(), writes=(), sig=True):
        pr = [k for k in reads if k.startswith("P_")]
        if pr:
            reads = [k for k in reads if not k.startswith("P_")]
            writes = list(writes) + [k for k in pr if k not in writes]
        self._deps(eng, reads, writes, False)
        ins = fn(self.E[eng])
        if eng == "pe":
            tok = self.pe_cur
            if sig:
                self.ccnt["pe"] += 1
                ins.then_inc(self.csem["pe"], 1)
                tok.val = self.ccnt["pe"]
                self.pe_cur = Tok(self.csem["pe"], None, "pe", "c")
                self.pe_unsig = 0
            else:
                self.pe_unsig += 1
        else:
            self.ccnt[eng] += 1
            ins.then_inc(self.csem[eng], 1)
            tok = Tok(self.csem[eng], self.ccnt[eng], eng, "c")
        self._record(tok, reads, writes)
        return tok

    def dma(self, q, out, in_, reads=(), writes=()):
        self._deps(q, reads, writes, True)
        i = self.dn[q] % len(self.dpool[q])
        self.dn[q] += 1
        self.dcnt[q][i] += 16
        ins = self.E[q].dma_start(out=out, in_=in_)
        ins.then_inc(self.dpool[q][i], 16)
        tok = Tok(self.dpool[q][i], self.dcnt[q][i], q, "d")
        self._record(tok, reads, writes)
        return tok

    def allgather(self, in_ap, out_ap, groups, reads=(), writes=()):
        self._deps("pool", reads, writes, True)
        self.cccnt += 1
        ins = self.E["pool"].collective_compute("AllGather", ALU.bypass, replica_groups=groups, ins=[in_ap], outs=[out_ap])
        ins.then_inc(self.ccsem, 1)
        tok = Tok(self.ccsem, self.cccnt, "pool", "d")
        self._record(tok, reads, writes)
        return tok

    def barrier(self):
        assert self.pe_unsig == 0, "last PE op before barrier must be signaled"
        toks = [Tok(self.csem[e], self.ccnt[e], e, "c") for e in self.csem if self.ccnt[e] > 0]
        for q in self.dpool:
            for i, s in enumerate(self.dpool[q]):
                if self.dcnt[q][i] > 0:
                    toks.append(Tok(s, self.dcnt[q][i], q, "d"))
        if self.cccnt:
            toks.append(Tok(self.ccsem, self.cccnt, "pool", "d"))
        for eng in ["pe", "act", "dve", "pool", "sp"]:
            for t in toks:
                self._wait(eng, t)
        self.lastw.clear()
        self.readers.clear()


def dram_bcast(handle, nparts, n, offset=0):
    return bass.AP(handle, offset, [[0, nparts], [1, n]])


class Cfg:
    def __init__(self, LP, LS, D=4096):
        self.LP, self.LS, self.D = LP, LS, D
        self.NP = LP // 8
        self.NS = LS // 4
        self.NT = self.NP + self.NS
        self.TT = min(512, self.NP, self.NS)
        assert self.NP % self.TT == 0 and self.NS % self.TT == 0 and self.TT % 128 == 0


def build(cfg, debug=False, zero_rest=False, stop=None):
    NP, NS, NT, TT, LP, LS = cfg.NP, cfg.NS, cfg.NT, cfg.TT, cfg.LP, cfg.LS
    SUB = TT // 128
    D = cfg.D
    KC = D // 128
    nc = bass.Bass("TRN2", target_bir_lowering=False)

    def din(name, shape, dt=F32):
        return nc.dram_tensor(name, list(shape), dt, kind="ExternalInput")

    def dscr(name, shape, dt=BF16):
        return nc.dram_tensor(name, list(shape), dt)

    x_own = din("x_own", [NT, D])
    mem_p = din("mem_p", [256, D])
    mem_s = din("mem_s", [256, D])
    w_in = din("w_in", [D, INW])
    w_out = din("w_out", [4096, D])
    w_glu = din("w_glu", [1024, 1024])
    w_mkv = din("w_mkv", [D, 2048])
    qg_in = din("qg", [128, 1])
    kg_in = din("kg", [128, 1])
    lam_re = din("lam_re", [64, 128])
    lam_im = din("lam_im", [64, 128])
    log_step = din("log_step", [1, 128])
    b_re = din("b_re", [2, 64, 64, 16])
    b_im = din("b_im", [2, 64, 64, 16])
    c_re = din("c_re", [64, 16, 64])
    c_im = din("c_im", [64, 16, 64])
    ssm_d = din("ssm_d", [32, 32])
    b_glu = din("b_glu", [8, 128])
    ln_g = din("ln_g", [1, D])
    ln_b = din("ln_b", [1, D])
    pos_row = din("pos_row", [1, NT])
    pos_col = din("pos_col", [1, NT])
    inv_col = din("inv_col", [128, 1])
    ident_bf_in = din("ident_bf", [128, 128], BF16)
    ident_f_in = din("ident_f", [128, 128])
    rot_in = din("rotm", [128, 128], BF16)
    selp = din("selp", [1, 16])
    sels = din("sels", [1, 8])

    y_out = nc.dram_tensor("y_out", [NT, D], F32, kind="ExternalOutput")

    w_in_bf = dscr("w_in_bf", [D, INW])
    w_out_bf = dscr("w_out_bf", [4096, D])
    w_glu_bf = dscr("w_glu_bf", [1024, 1024])
    w_mkv_bf = dscr("w_mkv_bf", [D, 2048])
    qT = dscr("qT", [2048, NT])
    kTp_own = dscr("kTp_own", [512, NP])
    kTs_own = dscr("kTs_own", [512, NS])
    vp_own = dscr("vp_own", [NP, 512])
    vs_own = dscr("vs_own", [NS, 512])
    kTp_all = dscr("kTp_all", [8 * 512, NP])
    kTs_all = dscr("kTs_all", [4 * 512, NS])
    vp_all = dscr("vp_all", [8 * NP, 512])
    vs_all = dscr("vs_all", [4 * NS, 512])
    sgatt = dscr("sgatt", [2048, NT])
    uT = dscr("uT", [1024, NT])
    sgssm = dscr("sgssm", [1024, NT])
    qmT = dscr("qmT", [1024, NT])
    sgmem = dscr("sgmem", [1024, NT])
    mixT = dscr("mixT", [4096, NT])
    dbg = {}
    if debug:
        for nm, src in [("qT", qT), ("kTp_all", kTp_all), ("vp_all", vp_all), ("mixT", mixT), ("sgatt", sgatt), ("uT", uT)]:
            dbg[nm] = (nc.dram_tensor("dbg_" + nm, list(src.shape), BF16, kind="ExternalOutput"), src)

    kb = KB(nc)
    ALL8 = [list(range(8))]
    GRP4 = [[0, 1, 2, 3], [4, 5, 6, 7]]

    kb.begin_stage()
    for dst, src, rows, ch in [(w_in_bf, w_in, D, 128), (w_out_bf, w_out, 4096, 256), (w_glu_bf, w_glu, 1024, 256), (w_mkv_bf, w_mkv, D, 512)]:
        for r0 in range(0, rows, ch):
            kb.dma("pool", dst.ap()[r0:r0 + ch, :], src.ap()[r0:r0 + ch, :])
    kb.end_stage()

    if stop == "0":
        kb.stack.close()
        return nc
    kb.begin_stage()
    ident_bf = kb.tile("ident_bf", [128, 128], BF16)
    rotm = kb.tile("rotm", [128, 128], BF16)
    ones_bf = kb.tile("ones_bf", [128, 128], BF16)
    qcol = kb.tile("qcol", [128, 1], F32)
    kcol = kb.tile("kcol", [128, 1], F32)
    invc = kb.tile("invc", [128, 1], F32)
    cosT = kb.tile("cosT", [128, NT], F32)
    sinT = kb.tile("sinT", [128, NT], F32)
    angT = kb.tile("angT", [128, NT], F32)
    kb.dma("sp", ident_bf[:], ident_bf_in.ap(), writes=["ident"])
    kb.dma("sp", rotm[:], rot_in.ap(), writes=["rotm"])
    kb.dma("sp", qcol[:], qg_in.ap(), writes=["qcol"])
    kb.dma("sp", kcol[:], kg_in.ap(), writes=["kcol0"])
    kb.dma("sp", invc[:], inv_col.ap(), writes=["invc"])
    kb.dma("sp", cosT[0:64, :], dram_bcast(pos_row, 64, NT), writes=["posA"])
    kb.dma("sp", cosT[64:128, :], dram_bcast(pos_col, 64, NT), writes=["posB"])
    kb.op("pool", lambda e: e.memset(ones_bf[:], 1.0), writes=["ones"])
    epsq = kb.tile("epsq", [128, 1], F32)
    kb.op("pool", lambda e: e.memset(epsq[:], 128.0 * 1e-6), writes=["epsq"])
    kb.op("dve", lambda e: e.tensor_scalar(out=kcol[:], in0=kcol[:], scalar1=math.sqrt(128.0), scalar2=None, op0=ALU.mult),
          reads=["kcol0"], writes=["kcol"])
    yT = kb.tile("yT", [128, NT], F32)
    kiT = kb.tile("kiT", [128, NT], mybir.dt.int32)
    kb.op("dve", lambda e: e.tensor_scalar(out=yT[:], in0=cosT[:], scalar1=invc[:, 0:1], scalar2=1.0 / TWO_PI, op0=ALU.mult, op1=ALU.mult),
          reads=["posA", "posB", "invc"], writes=["yT"])

    def trig_table(dst, shift, tag):
        kb.op("dve", lambda e: e.tensor_scalar(out=angT[:], in0=yT[:], scalar1=shift, scalar2=None, op0=ALU.add), reads=["yT", "trig"], writes=["angT"])
        kb.op("dve", lambda e: e.tensor_copy(out=kiT[:], in_=angT[:]), reads=["angT"], writes=["kiT"])
        kb.op("dve", lambda e: e.tensor_copy(out=dst[:], in_=kiT[:]), reads=["kiT"], writes=["kfT"])
        kb.op("dve", lambda e: e.tensor_tensor(out=angT[:], in0=angT[:], in1=dst[:], op=ALU.subtract), reads=["angT", "kfT"], writes=["angT"])
        kb.op("dve", lambda e: e.scalar_tensor_tensor(out=dst[:], in0=angT[:], scalar=0.5, in1=angT[:], op0=ALU.is_gt, op1=ALU.subtract), reads=["angT"], writes=["kfT"])
        kb.op("dve", lambda e: e.scalar_tensor_tensor(out=angT[:], in0=dst[:], scalar=0.5, in1=dst[:], op0=ALU.is_gt, op1=ALU.subtract), reads=["kfT"], writes=["angT"])
        kb.op("act", lambda e: e.activation(out=dst[:], in_=angT[:], func=AF.Sin, scale=TWO_PI), reads=["angT"], writes=[tag, "trig"])

    trig_table(sinT, 0.0, "sinT")
    trig_table(cosT, 0.25, "cosT")

    if stop == "A0":
        kb.end_stage(); kb.stack.close()
        return nc
    xb = kb.tile("xb", [128, SUB, D], BF16)
    xT = kb.tile("xT", [128, KC, TT], BF16)
    wbuf = [kb.tile(f"wbuf{i}", [128, KC, 512], BF16) for i in range(2)]
    pst = [kb.psum(f"pst{i}", [128, 1024], BF16) for i in range(2)]
    acc = [kb.psum(f"acc{i}", [128, 512], F32) for i in range(2)]
    ssps = kb.psum("P_ssps", [128, 512], F32)
    rotps = kb.psum("P_rotps", [128, 512], F32)
    sq = [kb.tile(f"sq{i}", [128, TT], BF16) for i in range(2)]
    qgb = [kb.tile(f"qgb{i}", [128, TT], BF16) for i in range(2)]
    rr = [kb.tile(f"rr{i}", [128, TT], F32) for i in range(2)]
    At = [kb.tile(f"At{i}", [128, TT], F32) for i in range(2)]
    Bt = [kb.tile(f"Bt{i}", [128, TT], F32) for i in range(1)]
    ob = [kb.tile(f"ob{i}", [128, 512], BF16) for i in range(3)]
    w_in_v = w_in_bf.ap().rearrange("(k p) c -> p k c", p=128)

    tiles = [(t0, 0) for t0 in range(0, NP, TT)] + [(t0, 1) for t0 in range(NP, NT, TT)]
    nacc = 0
    nob = 0
    nqk = 0
    ngrp = 0
    for (t0, seg) in tiles:
        for s in range(SUB):
            kb.dma("pool", xb[:, s, :], x_own.ap()[t0 + 128 * s: t0 + 128 * (s + 1), :], writes=[f"xb{s}"])
        for k in range(KC):
            pb = k % 2
            for s in range(SUB):
                kb.op("pe", lambda e, k=k, s=s, pb=pb: e.transpose(out=pst[pb][:, 128 * s:128 * (s + 1)], in_=xb[:, s, 128 * k:128 * (k + 1)], identity=ident_bf[:]),
                      reads=[f"xb{s}", "ident"], writes=[f"P_pst{pb}"], sig=(s == SUB - 1))
            eng = "dve" if k % 2 == 0 else "act"
            if eng == "dve":
                kb.op("dve", lambda e, k=k, pb=pb: e.tensor_copy(out=xT[:, k, :], in_=pst[pb][:, 0:TT]), reads=[f"P_pst{pb}"], writes=[f"xT{k}"])
            else:
                kb.op("act", lambda e, k=k, pb=pb: e.activation(out=xT[:, k, :], in_=pst[pb][:, 0:TT], func=AF.Copy), reads=[f"P_pst{pb}"], writes=[f"xT{k}"])
        if stop == "A1":
            kb.end_stage(); kb.stack.close()
            return nc
        for g in range(18):
            wb = wbuf[ngrp % 2]
            wkey = f"wbuf{ngrp % 2}"
            if ngrp == 0:
                kb.dma("sp", wb[:], w_in_v[:, :, 0:512], writes=[wkey])
            ngrp += 1
            if not (g == 17 and (t0, seg) == tiles[-1]):
                gn = (g + 1) % 18
                kb.dma("sp", wbuf[ngrp % 2][:], w_in_v[:, :, 512 * gn:512 * (gn + 1)], writes=[f"wbuf{ngrp % 2}"])
            if g == 5:
                for s in range(SUB):
                    a = acc[nacc % 2]
                    akey = f"P_acc{nacc % 2}"
                    nacc += 1
                    for k in range(KC):
                        kb.op("pe", lambda e, a=a, k=k, s=s, wb=wb: e.matmul(a[:], lhsT=xT[:, k, 128 * s:128 * (s + 1)], rhs=wb[:, k, :], start=(k == 0), stop=(k == KC - 1)),
                              reads=[f"xT{k}", wkey], writes=[akey], sig=(k == KC - 1))
                    o = ob[nob % 3]
                    okey = f"ob{nob % 3}"
                    nob += 1
                    kb.op("dve", lambda e, a=a, o=o: e.tensor_copy(out=o[:], in_=a[:]), reads=[akey], writes=[okey])
                    r0 = t0 + 128 * s
                    dst = vp_own.ap()[r0:r0 + 128, :] if seg == 0 else vs_own.ap()[r0 - NP:r0 - NP + 128, :]
                    kb.dma("sp", dst, o[:], reads=[okey])
                continue
            for j in range(4):
                col0 = 512 * g + 128 * j
                a = acc[nacc % 2]
                akey = f"P_acc{nacc % 2}"
                nacc += 1
                for k in range(KC):
                    kb.op("pe", lambda e, a=a, k=k, j=j, wb=wb: e.matmul(a[:, 0:TT], lhsT=wb[:, k, 128 * j:128 * (j + 1)], rhs=xT[:, k, :], start=(k == 0), stop=(k == KC - 1)),
                          reads=[f"xT{k}", wkey], writes=[akey], sig=(k == KC - 1))
                o = ob[nob % 3]
                okey = f"ob{nob % 3}"
                nob += 1
                if g <= 4:
                    i2 = nqk % 2
                    nqk += 1
                    gcol = qcol if g < 4 else kcol
                    gkey = "qcol" if g < 4 else "kcol"
                    kb.op("act", lambda e, a=a, i2=i2: e.activation(out=sq[i2][:], in_=a[:, 0:TT], func=AF.Square), reads=[akey], writes=[f"sq{i2}"])
                    kb.op("act", lambda e, a=a, i2=i2, gcol=gcol: e.activation(out=qgb[i2][:], in_=a[:, 0:TT], func=AF.Copy, scale=gcol[:, 0:1]),
                          reads=[akey, gkey], writes=[f"qgb{i2}"])
                    kb.op("pe", lambda e, i2=i2: e.matmul(ssps[:, 0:TT], lhsT=ones_bf[:], rhs=sq[i2][:], start=True, stop=True),
                          reads=[f"sq{i2}", "ones"], writes=["P_ssps"])
                    kb.op("pe", lambda e, i2=i2: e.matmul(rotps[:, 0:TT], lhsT=rotm[:], rhs=qgb[i2][:], start=True, stop=True),
                          reads=[f"qgb{i2}", "rotm"], writes=["P_rotps"])
                    kb.op("act", lambda e, i2=i2: e.activation(out=rr[i2][:], in_=ssps[:, 0:TT], func=AF.Sqrt, bias=epsq[:, 0:1]), reads=["P_ssps", "epsq"], writes=[f"rr{i2}"])
                    kb.op("dve", lambda e, i2=i2: e.reciprocal(out=rr[i2][:], in_=rr[i2][:]), reads=[f"rr{i2}"], writes=[f"rr{i2}"])
                    kb.op("dve", lambda e, a=a, i2=i2, gcol=gcol: e.scalar_tensor_tensor(out=At[i2][:], in0=a[:, 0:TT], scalar=gcol[:, 0:1], in1=cosT[:, t0:t0 + TT], op0=ALU.mult, op1=ALU.mult),
                          reads=[akey, gkey, "cosT"], writes=[f"At{i2}"])
                    kb.op("dve", lambda e, i2=i2: e.tensor_tensor(out=Bt[0][:], in0=rotps[:, 0:TT], in1=sinT[:, t0:t0 + TT], op=ALU.mult),
                          reads=["P_rotps", "sinT"], writes=["Bt0"])
                    kb.op("pool", lambda e, i2=i2: e.tensor_tensor(out=At[i2][:], in0=At[i2][:], in1=Bt[0][:], op=ALU.add),
                          reads=[f"At{i2}", "Bt0"], writes=[f"At{i2}"])
                    kb.op("pool", lambda e, i2=i2, o=o: e.tensor_tensor(out=o[:, 0:TT], in0=At[i2][:], in1=rr[i2][:], op=ALU.mult),
                          reads=[f"At{i2}", f"rr{i2}"], writes=[okey])
                    if g < 4:
                        dst = qT.ap()[col0:col0 + 128, t0:t0 + TT]
                        wk = "qT"
                    else:
                        r0 = 128 * j
                        dst = kTp_own.ap()[r0:r0 + 128, t0:t0 + TT] if seg == 0 else kTs_own.ap()[r0:r0 + 128, t0 - NP:t0 - NP + TT]
                        wk = "k_own"
                    kb.dma("sp", dst, o[:, 0:TT], reads=[okey])
                    continue
                if 6 <= g <= 9:
                    dst, fn, sc = sgatt.ap()[col0 - 3072:col0 - 3072 + 128, t0:t0 + TT], AF.Silu, 1.0
                elif 10 <= g <= 11:
                    dst, fn, sc = uT.ap()[col0 - 5120:col0 - 5120 + 128, t0:t0 + TT], AF.Copy, 1.0
                elif 12 <= g <= 13:
                    dst, fn, sc = sgssm.ap()[col0 - 6144:col0 - 6144 + 128, t0:t0 + TT], AF.Silu, 1.0
                elif 14 <= g <= 15:
                    dst, fn, sc = qmT.ap()[col0 - 7168:col0 - 7168 + 128, t0:t0 + TT], AF.Copy, 1.0 / 16.0
                else:
                    dst, fn, sc = sgmem.ap()[col0 - 8192:col0 - 8192 + 128, t0:t0 + TT], AF.Silu, 1.0
                if fn == AF.Copy:
                    kb.op("dve", lambda e, a=a, o=o, sc=sc: e.tensor_scalar(out=o[:, 0:TT], in0=a[:, 0:TT], scalar1=sc, scalar2=None, op0=ALU.mult), reads=[akey], writes=[okey])
                else:
                    kb.op("act", lambda e, a=a, o=o, fn=fn: e.activation(out=o[:, 0:TT], in_=a[:, 0:TT], func=fn), reads=[akey], writes=[okey])
                kb.dma("sp", dst, o[:, 0:TT], reads=[okey])
    kb.end_stage()

    if stop == "A":
        kb.stack.close()
        return nc
    kb.begin_stage()
    kb.allgather(kTp_own.ap(), kTp_all.ap(), ALL8)
    kb.allgather(vp_own.ap(), vp_all.ap(), ALL8)
    kb.allgather(kTs_own.ap(), kTs_all.ap(), GRP4)
    kb.allgather(vs_own.ap(), vs_all.ap(), GRP4)
    kb.end_stage()

    if stop == "AG":
        kb.stack.close()
        return nc
    kb.begin_stage()
    ones_bf = kb.tile("ones_bf", [128, 128], BF16)
    kb.op("pool", lambda e: e.memset(ones_bf[:], 1.0), writes=["ones"])
    LMAX = max(LP, LS)
    KTs = [kb.tile(f"KT{i}", [128, LMAX], BF16) for i in range(2)]
    VVs = [kb.tile(f"VV{i}", [128, LMAX // 128, 128], BF16) for i in range(2)]
    qb = [kb.tile(f"qb{i}", [128, TT], BF16) for i in range(2)]
    gb = [kb.tile(f"gb{i}", [128, TT], BF16) for i in range(2)]
    NPT = 6
    Pt = [kb.tile(f"Pt{i}", [128, TT], BF16) for i in range(NPT)]
    accD = [kb.tile(f"accD{i}", [128, TT], F32) for i in range(2)]
    accP = [kb.tile(f"accP{i}", [128, TT], F32) for i in range(2)]
    ones_f = kb.tile("ones_f", [128, 128], F32)
    kb.op("pool", lambda e: e.memset(ones_f[:], 1.0), writes=["ones_f"])
    rec = kb.tile("rec", [128, TT], F32)
    ot = kb.tile("ot", [128, TT], F32)
    ob2 = [kb.tile(f"ob2{i}", [128, TT], BF16) for i in range(2)]
    NSB = 4
    Sps = [kb.psum(f"Sps{i}", [128, 512], F32) for i in range(NSB)]
    Ops = [kb.psum(f"Ops{i}", [128, 512], F32) for i in range(2)]
    Lps = [kb.psum(f"Lps{i}", [128, 512], F32) for i in range(1)]
    nS = 0
    nP = 0
    heads = [(seg, h) for seg in range(2) for h in range(4)]
    blocks = []
    for hi, (seg, h) in enumerate(heads):
        nown = NP if seg == 0 else NS
        tbase = 0 if seg == 0 else NP
        for hq in range(4 * h, 4 * h + 4):
            for qb0 in range(0, nown, TT):
                blocks.append((hi, hq, tbase + qb0))

    def load_kv(hi):
        seg, h = heads[hi]
        L = LP if seg == 0 else LS
        nranks = 8 if seg == 0 else 4
        kall = kTp_all if seg == 0 else kTs_all
        vall = vp_all if seg == 0 else vs_all
        kb.dma("sp", KTs[hi % 2][:, 0:L].rearrange("p (r t) -> p r t", r=nranks),
               kall.ap().rearrange("(r hd) t -> hd r t", r=nranks)[128 * h:128 * (h + 1), :, :], writes=[f"KT{hi % 2}"])
        kb.dma("sp", VVs[hi % 2][:, 0:L // 128, :], vall.ap().rearrange("(kt p) c -> p kt c", p=128)[:, :, 128 * h:128 * (h + 1)], writes=[f"VV{hi % 2}"])

    def load_qg(bi_):
        _, hq_, tq_ = blocks[bi_]
        kb.dma("sp", qb[bi_ % 2][:], qT.ap()[128 * hq_:128 * (hq_ + 1), tq_:tq_ + TT], writes=[f"qb{bi_ % 2}"])
        kb.dma("sp", gb[bi_ % 2][:], sgatt.ap()[128 * hq_:128 * (hq_ + 1), tq_:tq_ + TT], writes=[f"gb{bi_ % 2}"])

    load_kv(0)
    load_qg(0)
    for nblk, (hi, hq, tq) in enumerate(blocks):
        seg, h = heads[hi]
        nkt = (LP if seg == 0 else LS) // 128
        first_of_head = (nblk == 0 or blocks[nblk - 1][0] != hi)
        if first_of_head and hi + 1 < len(heads):
            load_kv(hi + 1)
        if nblk + 1 < len(blocks):
            load_qg(nblk + 1)
        bi = nblk % 2
        KT, VV = KTs[hi % 2], VVs[hi % 2]
        ktk, vvk = f"KT{hi % 2}", f"VV{hi % 2}"
        qt, gt = qb[bi], gb[bi]
        O, Lp = Ops[bi], Lps[0]

        def emit_S(kt):
            nonlocal nS
            si = nS % NSB
            nS += 1
            kb.op("pe", lambda e: e.matmul(Sps[si][:, 0:TT], lhsT=KT[:, 128 * kt:128 * (kt + 1)], rhs=qt[:], start=True, stop=True),
                  reads=[ktk, f"qb{bi}"], writes=[f"P_Sps{si}"])
            return si

        def emit_PV(kt, si):
            nonlocal nP
            pi = nP % NPT
            nP += 1
            kb.op("act", lambda e: e.activation(out=Pt[pi][:], in_=Sps[si][:, 0:TT], func=AF.Exp), reads=[f"P_Sps{si}"], writes=[f"Pt{pi}"])
            kb.op("pe", lambda e: e.matmul(O[:, 0:TT], lhsT=VV[:, kt, :], rhs=Pt[pi][:], start=(kt == 0), stop=(kt == nkt - 1)),
                  reads=[vvk, f"Pt{pi}"], writes=[f"P_Ops{bi}"], sig=(kt == nkt - 1))
            if kt % 3 == 2:
                eng_, acc_, akey_, first_ = "pool", accP[bi], f"accP{bi}", (kt == 2)
            else:
                eng_, acc_, akey_, first_ = "dve", accD[bi], f"accD{bi}", (kt == 0)
            if first_:
                kb.op(eng_, lambda e: e.tensor_copy(out=acc_[:], in_=Pt[pi][:]), reads=[f"Pt{pi}"], writes=[akey_])
            else:
                kb.op(eng_, lambda e: e.tensor_tensor(out=acc_[:], in0=acc_[:], in1=Pt[pi][:], op=ALU.add), reads=[f"Pt{pi}", akey_], writes=[akey_])

        pend = [emit_S(k_) for k_ in range(min(2, nkt))]
        for kt in range(nkt):
            if kt + 2 < nkt:
                pend.append(emit_S(kt + 2))
            emit_PV(kt, pend.pop(0))
        oi = nblk % 2
        kb.op("pe", lambda e: e.matmul(Lp[:, 0:TT], lhsT=ones_f[:], rhs=accD[bi][:], start=True, stop=False),
              reads=["ones_f", f"accD{bi}"], writes=["P_Lps0"], sig=False)
        kb.op("pe", lambda e: e.matmul(Lp[:, 0:TT], lhsT=ones_f[:], rhs=accP[bi][:], start=False, stop=True),
              reads=["ones_f", f"accP{bi}"], writes=["P_Lps0"], sig=True)
        kb.op("dve", lambda e: e.reciprocal(out=rec[:], in_=Lp[:, 0:TT]), reads=["P_Lps0"], writes=["rec"])
        kb.op("dve", lambda e: e.tensor_tensor(out=ot[:], in0=O[:, 0:TT], in1=rec[:], op=ALU.mult), reads=[f"P_Ops{bi}", "rec"], writes=["ot"])
        kb.op("pool", lambda e: e.tensor_tensor(out=ob2[oi][:], in0=ot[:], in1=gt[:], op=ALU.mult), reads=["ot", f"gb{bi}"], writes=[f"ob2{oi}"])
        kb.dma("sp", mixT.ap()[128 * hq:128 * (hq + 1), tq:tq + TT], ob2[oi][:], reads=[f"ob2{oi}"])
    kb.end_stage()


    if zero_rest != True:
        kb.begin_stage()
        ones_bf = kb.tile("ones_bf", [128, 128], BF16)
        ident_bf = kb.tile("ident_bf", [128, 128], BF16)
        kb.op("pool", lambda e: e.memset(ones_bf[:], 1.0), writes=["ones"])
        kb.dma("sp", ident_bf[:], ident_bf_in.ap(), writes=["ident"])
        memb = kb.tile("memb", [128, 2, D], BF16)
        memT = kb.tile("memT", [128, KC, 256], BF16)
        wmk = [kb.tile(f"wmk{i}", [128, KC, 512], BF16) for i in range(2)]
        kmT = kb.tile("kmT", [128, 8, 256], BF16)
        vmt = kb.tile("vmt", [128, 2, 1024], BF16)
        qmt = kb.tile("qmt", [128, 8, TT], BF16)
        sgm = kb.tile("sgm", [128, 8, TT], BF16)
        Pm = [kb.tile(f"Pm{i}", [128, TT], BF16) for i in range(4)]
        recm = kb.tile("recm", [128, TT], F32)
        otm = kb.tile("otm", [128, TT], F32)
        obm = [kb.tile(f"obm{i}", [128, TT], BF16) for i in range(2)]
        mbank = [kb.psum(f"mb{i}", [128, 512], F32) for i in range(6)]
        tbank = kb.psum("mtb", [128, 1024], BF16)
        nmb = [0]

        def nb():
            i = nmb[0] % len(mbank)
            nmb[0] += 1
            return mbank[i], f"P_mb{i}"

        w_mkv_v = w_mkv_bf.ap().rearrange("(k p) c -> p k c", p=128)
        nwm = 0
        npm = 0
        nobm = 0
        for seg in range(2):
            memsrc = mem_p if seg == 0 else mem_s
            for s2 in range(2):
                kb.dma("pool", memb[:, s2, :], memsrc.ap()[128 * s2:128 * (s2 + 1), :], writes=[f"memb{s2}"])
            for k in range(KC):
                for s2 in range(2):
                    kb.op("pe", lambda e: e.transpose(out=tbank[:, 128 * s2:128 * (s2 + 1)], in_=memb[:, s2, 128 * k:128 * (k + 1)], identity=ident_bf[:]),
                          reads=[f"memb{s2}", "ident"], writes=["P_mtb"], sig=(s2 == 1))
                kb.op("dve", lambda e: e.tensor_copy(out=memT[:, k, :], in_=tbank[:, 0:256]), reads=["P_mtb"], writes=["memT"])
            for g in range(4):
                w = wmk[nwm % 2]
                wkey = f"wmk{nwm % 2}"
                nwm += 1
                kb.dma("sp", w[:], w_mkv_v[:, :, 512 * g:512 * (g + 1)], writes=[wkey])
                if g < 2:
                    for j in range(4):
                        bk, bkey = nb()
                        for k in range(KC):
                            kb.op("pe", lambda e: e.matmul(bk[:, 0:256], lhsT=w[:, k, 128 * j:128 * (j + 1)], rhs=memT[:, k, :], start=(k == 0), stop=(k == KC - 1)),
                                  reads=["memT", wkey], writes=[bkey], sig=(k == KC - 1))
                        kb.op("act", lambda e: e.activation(out=kmT[:, 4 * g + j, :], in_=bk[:, 0:256], func=AF.Copy), reads=[bkey], writes=["kmT"])
                else:
                    for m2 in range(2):
                        bk, bkey = nb()
                        for k in range(KC):
                            kb.op("pe", lambda e: e.matmul(bk[:], lhsT=memT[:, k, 128 * m2:128 * (m2 + 1)], rhs=w[:, k, :], start=(k == 0), stop=(k == KC - 1)),
                                  reads=["memT", wkey], writes=[bkey], sig=(k == KC - 1))
                        kb.op("dve", lambda e: e.tensor_copy(out=vmt[:, m2, 512 * (g - 2):512 * (g - 1)], in_=bk[:]), reads=[bkey], writes=["vmt"])
            for (t0, sg2) in tiles:
                if sg2 != seg:
                    continue
                kb.dma("sp", qmt[:], qmT.ap().rearrange("(c p) t -> p c t", p=128)[:, :, t0:t0 + TT], writes=["qmt"])
                kb.dma("sp", sgm[:], sgmem.ap().rearrange("(c p) t -> p c t", p=128)[:, :, t0:t0 + TT], writes=["sgm"])
                for h in range(4):
                    pk = []
                    for m2 in range(2):
                        bk, bkey = nb()
                        for c2 in range(2):
                            kb.op("pe", lambda e: e.matmul(bk[:, 0:TT], lhsT=kmT[:, 2 * h + c2, 128 * m2:128 * (m2 + 1)], rhs=qmt[:, 2 * h + c2, :], start=(c2 == 0), stop=(c2 == 1)),
                                  reads=["kmT", "qmt"], writes=[bkey], sig=(c2 == 1))
                        pi = npm % 4
                        npm += 1
                        kb.op("act", lambda e: e.activation(out=Pm[pi][:], in_=bk[:, 0:TT], func=AF.Exp), reads=[bkey], writes=[f"Pm{pi}"])
                        pk.append(pi)
                    lbk, lkey = nb()
                    for m2 in range(2):
                        kb.op("pe", lambda e: e.matmul(lbk[:, 0:TT], lhsT=ones_bf[:], rhs=Pm[pk[m2]][:], start=(m2 == 0), stop=(m2 == 1)),
                              reads=["ones", f"Pm{pk[m2]}"], writes=[lkey], sig=(m2 == 1))
                    kb.op("dve", lambda e: e.reciprocal(out=recm[:], in_=lbk[:, 0:TT]), reads=[lkey], writes=["recm"])
                    for c2 in range(2):
                        obk, okey2 = nb()
                        for m2 in range(2):
                            kb.op("pe", lambda e: e.matmul(obk[:, 0:TT], lhsT=vmt[:, m2, 256 * h + 128 * c2:256 * h + 128 * (c2 + 1)], rhs=Pm[pk[m2]][:], start=(m2 == 0), stop=(m2 == 1)),
                                  reads=["vmt", f"Pm{pk[m2]}"], writes=[okey2], sig=(m2 == 1))
                        kb.op("dve", lambda e: e.tensor_tensor(out=otm[:], in0=obk[:, 0:TT], in1=recm[:], op=ALU.mult), reads=[okey2, "recm"], writes=["otm"])
                        oi = nobm % 2
                        nobm += 1
                        kb.op("pool", lambda e: e.tensor_tensor(out=obm[oi][:], in0=otm[:], in1=sgm[:, 2 * h + c2, :], op=ALU.mult), reads=["otm", "sgm"], writes=[f"obm{oi}"])
                        r0 = 3072 + 256 * h + 128 * c2
                        kb.dma("sp", mixT.ap()[r0:r0 + 128, t0:t0 + TT], obm[oi][:], reads=[f"obm{oi}"])
        kb.end_stage()


    if not zero_rest:
        I32 = mybir.dt.int32
        KP = int(math.log2(NP))
        KS = int(math.log2(NS))
        KMAX = max(KP, KS)
        NQ = KMAX + 1
        PAD = max(NP, NS) // 2
        offs = [PAD, PAD + NP + PAD]
        lens = [NP, NS]
        LA = PAD + NP + PAD + NS + PAD
        WGd = dscr("WGd", [32, 8 * 64 * 2 * 128])
        KKd = dscr("KKd", [32, 32 * 15 * 32])
        RDd = dscr("RDd", [128, 2 * 32 * 8 * 2 * 32])
        DDd = dscr("DDd", [32, 32 * 32])
        Qd = dscr("Qd", [128, 3 * 64 * NQ], F32)
        Ep_own = dscr("Ep_own", [128, 128], F32)
        Es_own = dscr("Es_own", [128, 128], F32)
        Ep_all = dscr("Ep_all", [8 * 128, 128], F32)
        Es_all = dscr("Es_all", [4 * 128, 128], F32)
        XINd = dscr("XINd", [128, 256], F32)
        yT = dscr("yT", [1024, NT])

        kb.begin_stage()
        identf = kb.tile("identf", [128, 128], F32)
        kb.dma("sp", identf[:], ident_f_in.ap(), writes=["identf"])
        pb_ = [kb.psum(f"cp{i}", [128, 512], F32) for i in range(4)]
        ncp = [0]

        def nbk():
            i = ncp[0] % 4
            ncp[0] += 1
            return pb_[i], f"P_cp{i}"

        uid = [0]

        def T64(name):
            return kb.tile(name, [128, 64], F32)

        def tt(out, a, b, op, eng="dve"):
            uid[0] += 1
            kb.op(eng, lambda e: e.tensor_tensor(out=out, in0=a, in1=b, op=op), reads=["c0"], writes=["c0"])

        def ts(out, a, sc, op, eng="dve"):
            kb.op(eng, lambda e: e.tensor_scalar(out=out, in0=a, scalar1=sc, scalar2=None, op0=op), reads=["c0"], writes=["c0"])

        def actf(out, a, fn, scale=1.0):
            kb.op("act", lambda e: e.activation(out=out, in_=a, func=fn, scale=scale), reads=["c0"], writes=["c0"])

        LRn = kb.tile("LRn", [64, 128], F32)
        LIn = kb.tile("LIn", [64, 128], F32)
        LSb = kb.tile("LSb", [128, 128], F32)
        kb.dma("sp", LRn[:], lam_re.ap(), writes=["c0"])
        kb.dma("sp", LIn[:], lam_im.ap(), writes=["c0"])
        kb.dma("sp", LSb[:], dram_bcast(log_step, 128, 128), writes=["c0"])
        LR, LI, DT, MAG, SN, CS, AR, AI, T1, T2, T3, FR, FI = [T64(n) for n in ["LR", "LI", "DT", "MAG", "SN", "CS", "AR", "AI", "T1", "T2", "T3", "FR", "FI"]]
        KI = kb.tile("KI", [128, 64], I32)
        for src, dst in [(LRn, LR), (LIn, LI)]:
            bk, bkey = nbk()
            kb.op("pe", lambda e: e.transpose(out=bk[:, 0:64], in_=src[:], identity=identf[0:64, 0:64]), reads=["c0", "identf"], writes=[bkey, "c0"])
            kb.op("dve", lambda e: e.tensor_copy(out=dst[:], in_=bk[:, 0:64]), reads=[bkey, "c0"], writes=["c0"])
        lsv = LSb[:].rearrange("p (j g) -> p j g", g=2)
        actf(DT[0:64, :], lsv[0:64, :, 0], AF.Exp)
        actf(DT[64:128, :], lsv[64:128, :, 1], AF.Exp)
        tt(T1[:], LR[:], DT[:], ALU.mult)
        actf(MAG[:], T1[:], AF.Exp)
        tt(T1[:], LI[:], DT[:], ALU.mult)
        ts(T1[:], T1[:], 1.0 / TWO_PI, ALU.mult)

        def trig64(dst, shift):
            ts(T2[:], T1[:], shift, ALU.add)
            kb.op("dve", lambda e: e.tensor_copy(out=KI[:], in_=T2[:]), reads=["c0"], writes=["c0"])
            kb.op("dve", lambda e: e.tensor_copy(out=T3[:], in_=KI[:]), reads=["c0"], writes=["c0"])
            tt(T2[:], T2[:], T3[:], ALU.subtract)
            kb.op("dve", lambda e: e.scalar_tensor_tensor(out=T3[:], in0=T2[:], scalar=0.5, in1=T2[:], op0=ALU.is_gt, op1=ALU.subtract), reads=["c0"], writes=["c0"])
            kb.op("dve", lambda e: e.scalar_tensor_tensor(out=T2[:], in0=T3[:], scalar=0.5, in1=T3[:], op0=ALU.is_gt, op1=ALU.subtract), reads=["c0"], writes=["c0"])
            actf(dst[:], T2[:], AF.Sin, scale=TWO_PI)

        trig64(SN, 0.0)
        trig64(CS, 0.25)
        tt(AR[:], MAG[:], CS[:], ALU.mult)
        tt(AI[:], MAG[:], SN[:], ALU.mult)
        tt(T1[:], LR[:], LR[:], ALU.mult)
        tt(T2[:], LI[:], LI[:], ALU.mult)
        tt(T1[:], T1[:], T2[:], ALU.add)
        kb.op("dve", lambda e: e.reciprocal(out=T1[:], in_=T1[:]), reads=["c0"], writes=["c0"])
        ts(T2[:], AR[:], -1.0, ALU.add)
        tt(FR[:], T2[:], LR[:], ALU.mult)
        tt(T3[:], AI[:], LI[:], ALU.mult)
        tt(FR[:], FR[:], T3[:], ALU.add)
        tt(FR[:], FR[:], T1[:], ALU.mult)
        tt(FI[:], AI[:], LR[:], ALU.mult)
        tt(T3[:], T2[:], LI[:], ALU.mult)
        tt(FI[:], FI[:], T3[:], ALU.subtract)
        tt(FI[:], FI[:], T1[:], ALU.mult)
        Qall = kb.tile("Qall", [128, 3, 64, NQ], F32)
        kb.op("dve", lambda e: e.tensor_copy(out=Qall[:, 0, :, 0], in_=AR[:]), reads=["c0"], writes=["c0"])
        kb.op("dve", lambda e: e.tensor_copy(out=Qall[:, 1, :, 0], in_=AI[:]), reads=["c0"], writes=["c0"])
        for k in range(1, NQ):
            tt(T1[:], Qall[:, 0, :, k - 1], Qall[:, 0, :, k - 1], ALU.mult)
            tt(T2[:], Qall[:, 1, :, k - 1], Qall[:, 1, :, k - 1], ALU.mult)
            tt(Qall[:, 0, :, k], T1[:], T2[:], ALU.subtract)
            tt(T3[:], Qall[:, 0, :, k - 1], Qall[:, 1, :, k - 1], ALU.mult)
            ts(Qall[:, 1, :, k], T3[:], 2.0, ALU.mult)
        ts(Qall[:, 2, :, :], Qall[:, 1, :, :], -1.0, ALU.mult)
        kb.dma("sp", Qd.ap(), Qall[:].rearrange("p a j k -> p (a j k)"), reads=["c0"])
        BR = kb.tile("BR", [128, 64, 16], F32)
        BI = kb.tile("BI", [128, 64, 16], F32)
        BBR = kb.tile("BBR", [128, 64, 16], F32)
        BBI = kb.tile("BBI", [128, 64, 16], F32)
        T16 = kb.tile("T16", [128, 64, 16], F32)
        PBR = kb.tile("PBR", [128, 64, 16], F32)
        PBI = kb.tile("PBI", [128, 64, 16], F32)
        for g2 in range(2):
            for src, dst in [(b_re, BR), (b_im, BI)]:
                sv = src.ap().rearrange("d (gp g2) p c -> g2 d p gp c", g2=2)
                for d_ in range(2):
                    for q0 in range(0, 32, 8):
                        kb.dma("sp", dst[64 * g2:64 * (g2 + 1), 32 * d_ + q0:32 * d_ + q0 + 8, :], sv[g2, d_, :, q0:q0 + 8, :], writes=["c0"])

        def bc16(ap64):
            return ap64.unsqueeze(2).to_broadcast([128, 64, 16])

        def cmul16(outr, outi, ar_, ai_, br_, bi_):
            tt(outr, br_, bc16(ar_), ALU.mult)
            tt(T16[:], bi_, bc16(ai_), ALU.mult)
            tt(outr, outr, T16[:], ALU.subtract)
            tt(outi, bi_, bc16(ar_), ALU.mult)
            tt(T16[:], br_, bc16(ai_), ALU.mult)
            tt(outi, outi, T16[:], ALU.add)

        cmul16(BBR[:], BBI[:], FR[:], FI[:], BR[:], BI[:])
        CXf = kb.tile("CXf", [128, 2, 32, 32], F32)
        CXin = kb.tile("CXin", [32, 32, 128], F32)
        for part, srcC in enumerate([c_re, c_im]):
            kb.op("pool", lambda e: e.memset(CXin[:], 0.0), reads=["c0"], writes=["c0"])
            for g2 in range(2):
                kb.dma("sp", CXin[16 * g2:16 * (g2 + 1), :, 64 * g2:64 * (g2 + 1)],
                       srcC.ap().rearrange("(gp g2) co p -> g2 co gp p", g2=2)[g2], reads=["c0"], writes=["c0"])
            for g0 in range(0, 32, 16):
                bk, bkey = nbk()
                for gg in range(16):
                    kb.op("pe", lambda e: e.transpose(out=bk[:, 32 * gg:32 * (gg + 1)], in_=CXin[:, g0 + gg, :], identity=identf[0:32, 0:32]),
                          reads=["c0", "identf"], writes=[bkey], sig=(gg == 15))
                kb.op("act", lambda e: e.activation(out=CXf[:, part, g0:g0 + 16, :], in_=bk[:, :].rearrange("p (g c) -> p g c", g=16), func=AF.Copy, scale=(1.0 if part == 0 else -1.0)),
                      reads=[bkey, "c0"], writes=["c0"])
        PK = kb.tile("PK", [128, 2, 9, 64], F32)
        kb.op("pool", lambda e: e.memset(PK[:, 0, 0, :], 1.0), reads=["c0"], writes=["c0"])
        kb.op("pool", lambda e: e.memset(PK[:, 1, 0, :], 0.0), reads=["c0"], writes=["c0"])
        for k in range(1, 9):
            tt(T1[:], PK[:, 0, k - 1, :], AR[:], ALU.mult)
            tt(T2[:], PK[:, 1, k - 1, :], AI[:], ALU.mult)
            tt(PK[:, 0, k, :], T1[:], T2[:], ALU.subtract)
            tt(T1[:], PK[:, 0, k - 1, :], AI[:], ALU.mult)
            tt(T2[:], PK[:, 1, k - 1, :], AR[:], ALU.mult)
            tt(PK[:, 1, k, :], T1[:], T2[:], ALU.add)
        EXPT = [kb.tile(f"EXPT{i}", [128, 64, 32], F32) for i in range(2)]
        WGsb = kb.tile("WGsb", [32, 64, 2, 128], BF16)
        KKsb = kb.tile("KKsb", [32, 32, 15, 32], BF16)
        for part in range(2):
            kb.op("pool", lambda e: e.memset(EXPT[part][:], 0.0), reads=["c0"], writes=["c0"])
        WGd_v = WGd.ap().rearrange("p (k x) -> p k x", k=8)
        for k in range(8):
            cmul16(PBR[:], PBI[:], PK[:, 0, k, :], PK[:, 1, k, :], BBR[:], BBI[:])
            for part, srcB in enumerate([PBR, PBI]):
                kb.op("dve", lambda e: e.tensor_copy(out=EXPT[part][0:64, :, 0:16], in_=srcB[0:64, :, :]), reads=["c0"], writes=["c0"])
                kb.op("dve", lambda e: e.tensor_copy(out=EXPT[part][64:128, :, 16:32], in_=srcB[64:128, :, :]), reads=["c0"], writes=["c0"])
            for part in range(2):
                for j0 in range(0, 64, 4):
                    bk, bkey = nbk()
                    for jj in range(4):
                        kb.op("pe", lambda e: e.transpose(out=bk[0:32, 128 * jj:128 * (jj + 1)], in_=EXPT[part][:, j0 + jj, :], identity=identf[:]),
                              reads=["c0", "identf"], writes=[bkey], sig=(jj == 3))
                    kb.op("act", lambda e: e.activation(out=WGsb[:, j0:j0 + 4, part, :], in_=bk[0:32, :].rearrange("p (j c) -> p j c", j=4), func=AF.Copy), reads=[bkey, "wgd"], writes=["wg"])
            kb.dma("sp", WGd_v[:, k, :], WGsb[:].rearrange("p j a c -> p (j a c)"), reads=["wg"], writes=["wgd"])
            if k == 0:
                for g0 in range(0, 32, 16):
                    bk, bkey = nbk()
                    for gg in range(16):
                        gp_ = g0 + gg
                        ops_ = [(EXPT[0][:, gp_, :], CXf[:, 0, gp_, :]), (EXPT[1][:, gp_, :], CXf[:, 1, gp_, :]),
                                (EXPT[0][:, 32 + gp_, :], CXf[:, 0, gp_, :]), (EXPT[1][:, 32 + gp_, :], CXf[:, 1, gp_, :])]
                        for i_, (l_, r_) in enumerate(ops_):
                            kb.op("pe", lambda e: e.matmul(bk[0:32, 32 * gg:32 * (gg + 1)], lhsT=l_, rhs=r_, start=(i_ == 0), stop=(i_ == 3)),
                                  reads=["c0"], writes=[bkey], sig=(gg == 15 and i_ == 3))
                    kb.op("act", lambda e: e.activation(out=KKsb[:, g0:g0 + 16, 7, :], in_=bk[0:32, :].rearrange("p (g c) -> p g c", g=16), func=AF.Copy), reads=[bkey], writes=["kk"])
            else:
                for j0 in range(0, 64, 16):
                    d_ = j0 // 32
                    g0 = j0 % 32
                    idx = 7 + k if d_ == 0 else 7 - k
                    bk, bkey = nbk()
                    for gg in range(16):
                        j_ = j0 + gg
                        gp_ = g0 + gg
                        kb.op("pe", lambda e: e.matmul(bk[0:32, 32 * gg:32 * (gg + 1)], lhsT=EXPT[0][:, j_, :], rhs=CXf[:, 0, gp_, :], start=True, stop=False),
                              reads=["c0"], writes=[bkey], sig=False)
                        kb.op("pe", lambda e: e.matmul(bk[0:32, 32 * gg:32 * (gg + 1)], lhsT=EXPT[1][:, j_, :], rhs=CXf[:, 1, gp_, :], start=False, stop=True),
                              reads=["c0"], writes=[bkey], sig=(gg == 15))
                    kb.op("act", lambda e: e.activation(out=KKsb[:, g0:g0 + 16, idx, :], in_=bk[0:32, :].rearrange("p (g c) -> p g c", g=16), func=AF.Copy), reads=[bkey], writes=["kk"])
        kb.dma("sp", KKd.ap(), KKsb[:].rearrange("p g i c -> p (g i c)"), reads=["kk"])
        TA = kb.tile("TA", [128, 32, 32], F32)
        TB = kb.tile("TB", [128, 32, 32], F32)
        RDh = kb.tile("RDh", [128, 32, 8, 2, 32], BF16)
        RDd_v = RDd.ap().rearrange("p (d x) -> p d x", d=2)
        for d_ in range(2):
            for k in range(1, 9):
                pre = PK[:, 0, k, 32 * d_:32 * (d_ + 1)].unsqueeze(2).to_broadcast([128, 32, 32])
                pim = PK[:, 1, k, 32 * d_:32 * (d_ + 1)].unsqueeze(2).to_broadcast([128, 32, 32])
                tt(TA[:], CXf[:, 0, :, :], pre, ALU.mult)
                tt(TB[:], CXf[:, 1, :, :], pim, ALU.mult)
                kb.op("dve", lambda e: e.tensor_tensor(out=RDh[:, :, k - 1, 0, :], in0=TA[:], in1=TB[:], op=ALU.add), reads=["c0", "rdd"], writes=["c0", "rdh"])
                tt(TA[:], CXf[:, 1, :, :], pre, ALU.mult)
                tt(TB[:], CXf[:, 0, :, :], pim, ALU.mult)
                kb.op("dve", lambda e: e.tensor_tensor(out=RDh[:, :, k - 1, 1, :], in0=TA[:], in1=TB[:], op=ALU.subtract), reads=["c0", "rdd"], writes=["c0", "rdh"])
            kb.dma("sp", RDd_v[:, d_, :], RDh[:].rearrange("p g k a c -> p (g k a c)"), reads=["rdh"], writes=["rdd"])
        Dn = kb.tile("Dn", [32, 32], F32)
        Dt_ = kb.tile("Dt_", [32, 32], F32)
        DDsb = kb.tile("DDsb", [32, 32, 32], BF16)
        kb.dma("sp", Dn[:], ssm_d.ap(), writes=["dn"])
        bk, bkey = nbk()
        kb.op("pe", lambda e: e.transpose(out=bk[0:32, 0:32], in_=Dn[:], identity=identf[0:32, 0:32]), reads=["dn", "identf"], writes=[bkey])
        kb.op("dve", lambda e: e.tensor_copy(out=Dt_[:], in_=bk[0:32, 0:32]), reads=[bkey], writes=["dt_"])
        for gp in range(32):
            kb.op("dve", lambda e: e.tensor_scalar(out=DDsb[:, gp, :], in0=identf[0:32, 0:32], scalar1=Dt_[:, gp:gp + 1], scalar2=None, op0=ALU.mult),
                  reads=["dt_", "identf"], writes=["dd"])
        kb.dma("sp", DDd.ap(), DDsb[:].rearrange("p g c -> p (g c)"), reads=["dd"])
        kb.end_stage()

        if stop == "C0":
            kb.stack.close()
            return nc
        def ssm_pass(final):
            kb.begin_stage()
            NMP, NMS = NP // 8, NS // 8
            NM = NMP + NMS
            DL = max(NMP, NMS)
            PADm = DL // 2
            blk = DL + PADm
            moffs = [PADm, PADm + blk]
            mlens = [NMP, NMS]
            mbase = [0, NMP]
            LAm = PADm + 2 * blk + PADm

            def bv(t_, start):
                return t_[:, start:start + 2 * blk].rearrange("p (b x) -> p b x", b=2)[:, :, 0:DL]
            KM = int(math.log2(max(NMP, NMS)))
            DD = kb.tile("DD", [32, 32, 32], BF16)
            Q = kb.tile("Q", [128, 3, 64, NQ], F32)
            kb.dma("sp", DD[:].rearrange("p g c -> p (g c)"), DDd.ap(), writes=["DD"])
            kb.dma("sp", Q[:].rearrange("p a j k -> p (a j k)"), Qd.ap(), writes=["Q"])
            XIN = kb.tile("XIN", [128, 2, 2, 2, 32], F32)
            Eloc = kb.tile("Eloc", [128, 2, 2, 2, 32], F32)
            if final:
                kb.dma("sp", XIN[:].rearrange("p s d a g -> p (s d a g)"), XINd.ap(), writes=["XIN"])
            XA = [[kb.tile(f"XA{d}{i}", [128, LAm], F32) for i in range(2)] for d in range(2)]
            XB = [[kb.tile(f"XB{d}{i}", [128, LAm], F32) for i in range(2)] for d in range(2)]
            for d in range(2):
                for i in range(2):
                    kb.op("pool", lambda e: e.memset(XA[d][i][:], 0.0), writes=[f"xa{d}{i}"])
                    kb.op("pool", lambda e: e.memset(XB[d][i][:], 0.0), writes=[f"xb{d}{i}"])
            Xbf = kb.tile("Xbf", [128, 2, 2, NM], BF16)
            U32 = [kb.tile(f"U32{i}", [32, NT], BF16) for i in range(2)]
            Y32 = [kb.tile(f"Y32{i}", [32, NT], BF16) for i in range(2)]
            WGp = [kb.tile(f"WGp{i}", [32, 8, 2, 2, 128], BF16) for i in range(2)]
            RDp = [kb.tile(f"RDp{i}", [128, 2, 8, 2, 32], BF16) for i in range(2)]
            KKp = [kb.tile(f"KKp{i}", [32, 15, 32], BF16) for i in range(2)]
            zps = [kb.psum(f"zps{i}", [128, 512], F32) for i in range(4)]
            yps = [kb.psum(f"yps{i}", [128, 512], F32) for i in range(2)]
            nz = 0
            ny = 0
            WGd_v = WGd.ap().rearrange("p (k d g a c) -> p k d g a c", k=8, d=2, g=32, a=2)
            RDd_v = RDd.ap().rearrange("p (d g x) -> p d g x", d=2, g=32)
            KKd_v = KKd.ap().rearrange("p (g x) -> p g x", g=32)
            for gp in range(32):
                b2 = gp % 2
                u = U32[b2]
                ukey = f"U32{b2}"
                kb.dma("sp", u[:], uT.ap()[32 * gp:32 * (gp + 1), :], writes=[ukey])
                kb.dma("sp", WGp[b2][:], WGd_v[:, :, :, gp, :, :], writes=[f"WGp{b2}"])
                if final:
                    kb.dma("sp", RDp[b2][:].rearrange("p d k a c -> p d (k a c)"), RDd_v[:, :, gp, :], writes=[f"RDp{b2}"])
                    kb.dma("sp", KKp[b2][:].rearrange("p i c -> p (i c)"), KKd_v[:, gp, :], writes=[f"KKp{b2}"])
                uv = u[:].rearrange("p (m t) -> p m t", t=8)
                for d in range(2):
                    j = 32 * d + gp
                    for part in range(2):
                        zi = nz % 4
                        nz += 1
                        for tp in range(8):
                            k_ = 7 - tp if d == 0 else tp
                            kb.op("pe", lambda e: e.matmul(zps[zi][:, 0:NM], lhsT=WGp[b2][:, k_, d, part, :], rhs=uv[:, :, tp], start=(tp == 0), stop=(tp == 7)),
                                  reads=[f"WGp{b2}", ukey], writes=[f"P_zps{zi}"], sig=(tp == 7))
                        for sg in range(2):
                            kb.op("act", lambda e: e.activation(out=XA[d][part][:, moffs[sg]:moffs[sg] + mlens[sg]], in_=zps[zi][:, mbase[sg]:mbase[sg] + mlens[sg]], func=AF.Copy),
                                  reads=[f"P_zps{zi}"], writes=[f"xa{d}{part}"])
                    if final:
                        for sg in range(2):
                            slot = moffs[sg] - 1 if d == 0 else moffs[sg] + mlens[sg]
                            for part in range(2):
                                kb.op("pool", lambda e: e.tensor_copy(out=XA[d][part][:, slot:slot + 1], in_=XIN[:, sg, d, part, gp:gp + 1]), reads=["XIN"], writes=[f"xa{d}{part}"])
                                kb.op("pool", lambda e: e.tensor_copy(out=XB[d][part][:, slot:slot + 1], in_=XIN[:, sg, d, part, gp:gp + 1]), reads=["XIN"], writes=[f"xb{d}{part}"])
                    src, dst, sk, dk = XA[d], XB[d], f"xa{d}", f"xb{d}"
                    for k in range(KM):
                        sh = 1 << k
                        sgn = -sh if d == 0 else sh
                        qr = Q[:, 0, j, k + 3:k + 4]
                        qi = Q[:, 1, j, k + 3:k + 4]
                        nqi = Q[:, 2, j, k + 3:k + 4]
                        A_ = moffs[0]
                        B_ = moffs[0] + sgn
                        kb.op("dve", lambda e: e.scalar_tensor_tensor(out=bv(dst[0], A_), in0=bv(src[0], B_), scalar=qr, in1=bv(src[0], A_), op0=ALU.mult, op1=ALU.add),
                              reads=[sk + "0", sk + "1", "Q"], writes=[dk + "0"])
                        kb.op("dve", lambda e: e.scalar_tensor_tensor(out=bv(dst[0], A_), in0=bv(src[1], B_), scalar=nqi, in1=bv(dst[0], A_), op0=ALU.mult, op1=ALU.add),
                              reads=[sk + "0", sk + "1", "Q"], writes=[dk + "0"])
                        kb.op("dve", lambda e: e.scalar_tensor_tensor(out=bv(dst[1], A_), in0=bv(src[1], B_), scalar=qr, in1=bv(src[1], A_), op0=ALU.mult, op1=ALU.add),
                              reads=[sk + "0", sk + "1", "Q"], writes=[dk + "1"])
                        kb.op("dve", lambda e: e.scalar_tensor_tensor(out=bv(dst[1], A_), in0=bv(src[0], B_), scalar=qi, in1=bv(dst[1], A_), op0=ALU.mult, op1=ALU.add),
                              reads=[sk + "0", sk + "1", "Q"], writes=[dk + "1"])
                        src, dst, sk, dk = dst, src, dk, sk
                    if not final:
                        for sg in range(2):
                            p0 = moffs[sg] + mlens[sg] - 1 if d == 0 else moffs[sg]
                            for part in range(2):
                                kb.op("act", lambda e: e.activation(out=Eloc[:, sg, d, part, gp:gp + 1], in_=src[part][:, p0:p0 + 1], func=AF.Copy), reads=[sk + str(part)], writes=["Eloc"])
                    else:
                        shf = -1 if d == 0 else 1
                        for part in range(2):
                            for sg in range(2):
                                o, n = moffs[sg], mlens[sg]
                                kb.op("act", lambda e: e.activation(out=Xbf[:, d, part, mbase[sg]:mbase[sg] + n], in_=src[part][:, o + shf:o + shf + n], func=AF.Copy),
                                      reads=[sk + str(part)], writes=[f"Xbf{d}"])
                if final:
                    y = Y32[b2]
                    ykey = f"Y32{b2}"
                    yv = y[:].rearrange("p (m t) -> p m t", t=8)
                    for tau in range(8):
                        yi = ny % 2
                        ny += 1
                        mm = [(KKp[b2][:, tau - tp + 7, :], uv[:, :, tp]) for tp in range(8)]
                        mm += [(RDp[b2][:, 0, tau, pt, :], Xbf[:, 0, pt, :]) for pt in range(2)]
                        mm += [(RDp[b2][:, 1, 7 - tau, pt, :], Xbf[:, 1, pt, :]) for pt in range(2)]
                        mm += [(DD[:, gp, :], uv[:, :, tau])]
                        for i_, (l_, r_) in enumerate(mm):
                            kb.op("pe", lambda e: e.matmul(yps[yi][0:32, 0:NM], lhsT=l_, rhs=r_, start=(i_ == 0), stop=(i_ == len(mm) - 1)),
                                  reads=[f"KKp{b2}", f"RDp{b2}", "DD", "Xbf0", "Xbf1", ukey], writes=[f"P_yps{yi}"], sig=(i_ == len(mm) - 1))
                        kb.op("act", lambda e: e.activation(out=yv[:, :, tau], in_=yps[yi][0:32, 0:NM], func=AF.Gelu), reads=[f"P_yps{yi}"], writes=[ykey])
                    kb.dma("sp", yT.ap()[32 * gp:32 * (gp + 1), :], y[:], reads=[ykey])
            if not final:
                kb.dma("sp", Ep_own.ap(), Eloc[:, 0, :, :, :].rearrange("p d a g -> p (d a g)"), reads=["Eloc"])
                kb.dma("sp", Es_own.ap(), Eloc[:, 1, :, :, :].rearrange("p d a g -> p (d a g)"), reads=["Eloc"])
            kb.end_stage()

        ssm_pass(False)
        if stop == "C1":
            kb.stack.close()
            return nc
        kb.begin_stage()
        kb.allgather(Ep_own.ap(), Ep_all.ap(), ALL8)
        kb.allgather(Es_own.ap(), Es_all.ap(), GRP4)
        kb.end_stage()

        kb.begin_stage()
        Q = kb.tile("Q", [128, 3, 64, NQ], F32)
        kb.dma("sp", Q[:].rearrange("p a j k -> p (a j k)"), Qd.ap(), writes=["cc"])
        EP = kb.tile("EP", [128, 8, 2, 2, 32], F32)
        ES = kb.tile("ES", [128, 4, 2, 2, 32], F32)
        kb.dma("sp", EP[:].rearrange("p r d a g -> p r (d a g)"), Ep_all.ap().rearrange("(r p) c -> p r c", p=128), writes=["cc"])
        kb.dma("sp", ES[:].rearrange("p r d a g -> p r (d a g)"), Es_all.ap().rearrange("(r p) c -> p r c", p=128), writes=["cc"])
        SELP = kb.tile("SELP", [128, 16], F32)
        SELS = kb.tile("SELS", [128, 8], F32)
        kb.dma("sp", SELP[:], dram_bcast(selp, 128, 16), writes=["cc"])
        kb.dma("sp", SELS[:], dram_bcast(sels, 128, 8), writes=["cc"])
        XINo = kb.tile("XINo", [128, 2, 2, 2, 32], F32)
        kb.op("pool", lambda e: e.memset(XINo[:], 0.0), reads=["cc"], writes=["cc"])
        cr, ci, n1, n2, n3 = [kb.tile(n_, [128, 32], F32) for n_ in ["cr", "ci", "n1", "n2", "n3"]]

        def cop(fn, eng="dve"):
            kb.op(eng, fn, reads=["cc"], writes=["cc"])

        for sg, (Et, nr_, SEL, kk) in enumerate([(EP, 8, SELP, KP), (ES, 4, SELS, KS)]):
            for d in range(2):
                anr = Q[:, 0, 32 * d:32 * (d + 1), kk]
                ani = Q[:, 1, 32 * d:32 * (d + 1), kk]
                cop(lambda e: e.memset(cr[:], 0.0))
                cop(lambda e: e.memset(ci[:], 0.0))
                order = range(nr_) if d == 0 else range(nr_ - 1, -1, -1)
                for r in order:
                    er = Et[:, r, d, 0, :]
                    ei = Et[:, r, d, 1, :]
                    cop(lambda e: e.tensor_tensor(out=n1[:], in0=cr[:], in1=anr, op=ALU.mult))
                    cop(lambda e: e.tensor_tensor(out=n2[:], in0=ci[:], in1=ani, op=ALU.mult))
                    cop(lambda e: e.tensor_tensor(out=n1[:], in0=n1[:], in1=n2[:], op=ALU.subtract))
                    cop(lambda e: e.tensor_tensor(out=n2[:], in0=cr[:], in1=ani, op=ALU.mult))
                    cop(lambda e: e.tensor_tensor(out=n3[:], in0=ci[:], in1=anr, op=ALU.mult))
                    cop(lambda e: e.tensor_tensor(out=ci[:], in0=n2[:], in1=n3[:], op=ALU.add))
                    cop(lambda e: e.tensor_tensor(out=ci[:], in0=ci[:], in1=ei, op=ALU.add))
                    cop(lambda e: e.tensor_tensor(out=cr[:], in0=n1[:], in1=er, op=ALU.add))
                    sc = SEL[:, nr_ * d + r:nr_ * d + r + 1]
                    cop(lambda e: e.scalar_tensor_tensor(out=XINo[:, sg, d, 0, :], in0=cr[:], scalar=sc, in1=XINo[:, sg, d, 0, :], op0=ALU.mult, op1=ALU.add))
                    cop(lambda e: e.scalar_tensor_tensor(out=XINo[:, sg, d, 1, :], in0=ci[:], scalar=sc, in1=XINo[:, sg, d, 1, :], op0=ALU.mult, op1=ALU.add))
        kb.dma("sp", XINd.ap(), XINo[:].rearrange("p s d a g -> p (s d a g)"), reads=["cc"])
        kb.end_stage()

        if stop == "C2":
            kb.stack.close()
            return nc
        ssm_pass(True)
        if stop == "C3":
            kb.stack.close()
            return nc

        kb.begin_stage()
        wgl = kb.tile("wgl", [128, 8, 1024], BF16)
        bgl = kb.tile("bgl", [128, 8], F32)
        kb.dma("sp", wgl[:], w_glu_bf.ap().rearrange("(k p) c -> p k c", p=128), writes=["wgl"])
        for c_ in range(8):
            kb.dma("sp", bgl[:, c_:c_ + 1], b_glu.ap()[c_:c_ + 1, :].rearrange("o p -> p o"), writes=["bgl"])
        yt = kb.tile("yt", [128, 8, TT], BF16)
        sgs = kb.tile("sgs", [128, 8, TT], BF16)
        sig_ = [kb.tile(f"sig{i}", [128, TT], F32) for i in range(2)]
        og = [kb.tile(f"og{i}", [128, TT], BF16) for i in range(2)]
        gps = [kb.psum(f"gps{i}", [128, 512], F32) for i in range(2)]
        ng = 0
        for (t0, seg) in tiles:
            kb.dma("sp", yt[:], yT.ap().rearrange("(c p) t -> p c t", p=128)[:, :, t0:t0 + TT], writes=["yt"])
            kb.dma("sp", sgs[:], sgssm.ap().rearrange("(c p) t -> p c t", p=128)[:, :, t0:t0 + TT], writes=["sgs"])
            for c in range(8):
                gi = ng % 2
                ng += 1
                for k in range(8):
                    kb.op("pe", lambda e: e.matmul(gps[gi][:, 0:TT], lhsT=wgl[:, k, 128 * c:128 * (c + 1)], rhs=yt[:, k, :], start=(k == 0), stop=(k == 7)),
                          reads=["wgl", "yt"], writes=[f"P_gps{gi}"], sig=(k == 7))
                kb.op("act", lambda e: e.activation(out=sig_[gi][:], in_=gps[gi][:, 0:TT], func=AF.Sigmoid, bias=bgl[:, c:c + 1]), reads=[f"P_gps{gi}", "bgl"], writes=[f"sig{gi}"])
                kb.op("dve", lambda e: e.tensor_tensor(out=sig_[gi][:], in0=sig_[gi][:], in1=yt[:, c, :], op=ALU.mult), reads=[f"sig{gi}", "yt"], writes=[f"sig{gi}"])
                kb.op("pool", lambda e: e.tensor_tensor(out=og[gi][:], in0=sig_[gi][:], in1=sgs[:, c, :], op=ALU.mult), reads=[f"sig{gi}", "sgs"], writes=[f"og{gi}"])
                kb.dma("sp", mixT.ap()[2048 + 128 * c:2048 + 128 * (c + 1), t0:t0 + TT], og[gi][:], reads=[f"og{gi}"])
        kb.end_stage()

    if zero_rest:
        kb.begin_stage()
        zt = kb.tile("zt", [128, NT], BF16)
        kb.op("pool", lambda e: e.memset(zt[:], 0.0), writes=["zt"])
        for r0 in range(2048, 4096 if zero_rest == True else 3072, 128):
            kb.dma("sp", mixT.ap()[r0:r0 + 128, :], zt[:], reads=["zt"])
        kb.end_stage()

    if debug:
        kb.begin_stage()
        for nm, (dst, src) in dbg.items():
            kb.dma("sp", dst.ap(), src.ap())
        kb.end_stage()

    kb.begin_stage()
    ALPHA = 2.0 ** 0.25
    CW = 256
    mt = kb.tile("mt", [128, 32, TT], BF16)
    wo = [kb.tile(f"wo{i}", [128, 32, CW], BF16) for i in range(2)]
    hh = [kb.tile(f"hh{s}", [128, D], F32) for s in range(SUB)]
    lg = kb.tile("lg", [128, D], F32)
    lb = kb.tile("lb", [128, D], F32)
    NST = max(1, D // 512)
    stats = kb.tile("stats", [128, NST, 6], F32)
    mv = kb.tile("mv", [128, 2], F32)
    rstd = kb.tile("rstd", [128, 1], F32)
    epsl = kb.tile("epsl", [128, 1], F32)
    kb.op("pool", lambda e: e.memset(epsl[:], 1e-5), writes=["epsl"])
    hps = [kb.psum(f"hps{i}", [128, 512], F32) for i in range(4)]
    kb.dma("sp", lg[:], dram_bcast(ln_g, 128, D), writes=["lg"])
    kb.dma("sp", lb[:], dram_bcast(ln_b, 128, D), writes=["lb"])
    w_out_v = w_out_bf.ap().rearrange("(k p) c -> p k c", p=128)
    mix_v = mixT.ap().rearrange("(k p) t -> p k t", p=128)
    nwo = 0
    nh = 0
    for (t0, seg) in tiles:
        kb.dma("sp", mt[:], mix_v[:, :, t0:t0 + TT], writes=["mt"])
        for s in range(SUB):
            kb.dma("sp", hh[s][:], x_own.ap()[t0 + 128 * s:t0 + 128 * (s + 1), :], writes=[f"hh{s}"])
        for cg in range(D // CW):
            w = wo[nwo % 2]
            wkey = f"wo{nwo % 2}"
            nwo += 1
            kb.dma("sp", w[:], w_out_v[:, :, CW * cg:CW * (cg + 1)], writes=[wkey])
            for s in range(SUB):
                hp = hps[nh % len(hps)]
                hkey = f"P_hps{nh % len(hps)}"
                nh += 1
                for k in range(32):
                    kb.op("pe", lambda e, hp=hp, k=k, s=s, w=w: e.matmul(hp[:, 0:CW], lhsT=mt[:, k, 128 * s:128 * (s + 1)], rhs=w[:, k, :], start=(k == 0), stop=(k == 31)),
                          reads=["mt", wkey], writes=[hkey], sig=(k == 31))
                kb.op("dve", lambda e, hp=hp, s=s, cg=cg: e.scalar_tensor_tensor(out=hh[s][:, CW * cg:CW * (cg + 1)], in0=hh[s][:, CW * cg:CW * (cg + 1)], scalar=ALPHA, in1=hp[:, 0:CW], op0=ALU.mult, op1=ALU.add),
                      reads=[hkey, f"hh{s}"], writes=[f"hh{s}"])
        for s in range(SUB):
            h = hh[s]
            for c8 in range(NST):
                kb.op("dve", lambda e, h=h, c8=c8: e.bn_stats(out=stats[:, c8, :], in_=h[:, (D // NST) * c8:(D // NST) * (c8 + 1)]), reads=[f"hh{s}"], writes=["stats"])
            kb.op("dve", lambda e: e.bn_aggr(out=mv[:], in_=stats[:]), reads=["stats"], writes=["mv"])
            kb.op("act", lambda e: e.activation(out=rstd[:], in_=mv[:, 1:2], func=AF.Sqrt, bias=epsl[:, 0:1]), reads=["mv", "epsl"], writes=["rstd"])
            kb.op("dve", lambda e: e.reciprocal(out=rstd[:], in_=rstd[:]), reads=["rstd"], writes=["rstd"])
            kb.op("dve", lambda e, h=h: e.tensor_scalar(out=h[:], in0=h[:], scalar1=mv[:, 0:1], scalar2=rstd[:, 0:1], op0=ALU.subtract, op1=ALU.mult),
                  reads=[f"hh{s}", "mv", "rstd"], writes=[f"hh{s}"])
            kb.op("pool", lambda e, h=h: e.tensor_tensor(out=h[:], in0=h[:], in1=lg[:], op=ALU.mult), reads=[f"hh{s}", "lg"], writes=[f"hh{s}"])
            kb.op("pool", lambda e, h=h: e.tensor_tensor(out=h[:], in0=h[:], in1=lb[:], op=ALU.add), reads=[f"hh{s}", "lb"], writes=[f"hh{s}"])
            kb.dma("sp", y_out.ap()[t0 + 128 * s:t0 + 128 * (s + 1), :], h[:], reads=[f"hh{s}"])
    kb.end_stage()
    kb.stack.close()
    return nc


def host_consts(cfg, core):
    NP, NS, NT = cfg.NP, cfg.NS, cfg.NT
    tp = core * NP + np.arange(NP)
    ts = (core % 4) * NS + np.arange(NS)
    t = np.concatenate([tp, ts])
    pos_row = (t // 64).astype(np.float32)[None, :]
    pos_col = (t % 64).astype(np.float32)[None, :]
    inv = (10000.0 ** (-np.arange(0, 64, 2, dtype=np.float32) / 64.0)).astype(np.float32)
    inv_col = np.concatenate([inv, inv, inv, inv]).astype(np.float32)[:, None]
    rot = np.zeros((128, 128), np.float32)
    for s in range(2):
        for j in range(32):
            rot[s * 64 + 32 + j, s * 64 + j] = -1.0
            rot[s * 64 + j, s * 64 + 32 + j] = 1.0
    selp = np.zeros((1, 16), np.float32)
    if core - 1 >= 0:
        selp[0, core - 1] = 1.0
    if core + 1 < 8:
        selp[0, 8 + core + 1] = 1.0
    sels = np.zeros((1, 8), np.float32)
    cs = core % 4
    if cs - 1 >= 0:
        sels[0, cs - 1] = 1.0
    if cs + 1 < 4:
        sels[0, 4 + cs + 1] = 1.0
    return dict(pos_row=pos_row, pos_col=pos_col, inv_col=inv_col, rotm=rot.astype(NPBF),
                ident_bf=np.eye(128, dtype=np.float32).astype(NPBF), ident_f=np.eye(128, dtype=np.float32),
                selp=selp, sels=sels)


def make_in_maps(cfg, inputs):
    NP, NS = cfg.NP, cfg.NS
    f = lambda a: np.ascontiguousarray(np.asarray(a, dtype=np.float32))
    xp = f(inputs["x_prompt"])[0]
    xs = f(inputs["x_sample"])
    shared = dict(
        mem_p=f(inputs["mem_prompt"])[0], w_in=f(inputs["w_in"])[0], w_out=f(inputs["w_out"])[0],
        w_glu=f(inputs["w_glu"])[0], w_mkv=f(inputs["w_mem_kv"])[0],
        qg=f(inputs["q_norm_g"])[0][:, None], kg=f(inputs["k_norm_g"])[0][:, None],
        lam_re=f(inputs["ssm_lam_re"])[0].reshape(64, 128), lam_im=f(inputs["ssm_lam_im"])[0].reshape(64, 128),
        log_step=f(inputs["ssm_log_step"])[0].reshape(1, 128),
        b_re=f(inputs["ssm_b_re"])[0], b_im=f(inputs["ssm_b_im"])[0],
        c_re=f(inputs["ssm_c_re"])[0], c_im=f(inputs["ssm_c_im"])[0],
        ssm_d=f(inputs["ssm_d"])[0].reshape(32, 32), b_glu=f(inputs["b_glu"])[0].reshape(8, 128),
        ln_g=f(inputs["ln_g"]), ln_b=f(inputs["ln_b"]),
    )
    maps = []
    for c in range(NCORES):
        b = c // 4
        m = dict(shared)
        m["x_own"] = np.ascontiguousarray(np.concatenate([xp[c * NP:(c + 1) * NP], xs[b, (c % 4) * NS:(c % 4 + 1) * NS]], 0))
        m["mem_s"] = f(inputs["mem_sample"])[b]
        m.update(host_consts(cfg, c))
        maps.append(m)
    return maps


def assemble(cfg, results):
    NP, NS = cfg.NP, cfg.NS
    yp = np.concatenate([r["y_out"][:NP] for r in results], 0)[None]
    ys = np.stack([np.concatenate([results[4 * b + i]["y_out"][NP:] for i in range(4)], 0) for b in range(2)], 0)
    return yp.astype(np.float32), ys.astype(np.float32)


def kernel(**inputs):
    cfg = Cfg(inputs["x_prompt"].shape[1], inputs["x_sample"].shape[1], inputs["x_prompt"].shape[2])
    nc = build(cfg)
    maps = make_in_maps(cfg, inputs)
    res = run_bass_kernel_spmd(nc, maps, core_ids=list(range(NCORES)))
    return assemble(cfg, res.results)
```
